# Optimizing a Trainium2 kernel written in Bass

```python
import math
import jax, jax.numpy as jnp
from jax import lax
import numpy as np

D_MODEL = 1024
BATCH = 8
SEQ = 2048
DEPTH = 4

A_GROUPS = 4
A_GROUP_DIM = 128
A_WIDTH = A_GROUPS * A_GROUP_DIM
A_CHUNK = 128
DN_HEADS = 4
DN_HEAD_DIM = 128
DN_WIDTH = DN_HEADS * DN_HEAD_DIM
DN_CONV = 5
DN_CHUNK = 64
MLA_HEADS = 4
MLA_Q_RANK = 384
MLA_KV_RANK = 256
MLA_NOPE = 128
MLA_ROPE = 64
MLA_V = 128
MLA_WIDTH = MLA_HEADS * MLA_V
ROPE_THETA = 10000.0
Q_BLOCK = 128
MAX_POS_OFFSET = 4096
N_BRANCH = 3
BRANCH_WIDTH = 512
D_FF = 2816
FFN_CONV = 3
RMS_EPS = 1e-6
LN_EPS = 1e-5

IN_SIZES = (2 * A_WIDTH, 3 * DN_WIDTH, DN_WIDTH, 2 * DN_HEADS, 2 * DN_HEADS,
            MLA_Q_RANK, MLA_KV_RANK, MLA_ROPE, N_BRANCH * D_MODEL)
N_IN = 2 * A_WIDTH + 3 * DN_WIDTH + DN_WIDTH + 4 * DN_HEADS + MLA_Q_RANK + MLA_KV_RANK + MLA_ROPE + N_BRANCH * D_MODEL

kernel_name = "hybrid_gmlp_deltanet_mla_encoder"


def _split_sizes(t, sizes):
    idx = [int(i) for i in np.cumsum(sizes)[:-1]]
    return jnp.split(t, idx, axis=-1)


def _rmsnorm(x, w, eps=RMS_EPS):
    xf = x.astype(jnp.float32)
    y = xf * lax.rsqrt(jnp.mean(xf * xf, axis=-1, keepdims=True) + eps)
    return (y * w.astype(jnp.float32)).astype(x.dtype)


def _layernorm(x, w, b, eps=LN_EPS):
    xf = x.astype(jnp.float32)
    mu = jnp.mean(xf, axis=-1, keepdims=True)
    var = jnp.mean(jnp.square(xf - mu), axis=-1, keepdims=True)
    y = (xf - mu) * lax.rsqrt(var + eps)
    return (y * w.astype(jnp.float32) + b.astype(jnp.float32)).astype(x.dtype)


def _l2norm(x, eps=1e-6):
    return x * lax.rsqrt(jnp.sum(x * x, axis=-1, keepdims=True) + eps)


def _dwconv(x, w, pad):
    return lax.conv_general_dilated(
        x, w[:, None, :].astype(x.dtype), window_strides=(1,), padding=[(pad, pad)],
        dimension_numbers=('NWC', 'WIO', 'NWC'), feature_group_count=x.shape[-1])


def _spatial_gating(a_uv, ln_w, ln_b, w_sp, b_sp):
    bn, s, _ = a_uv.shape
    u, v = jnp.split(jax.nn.gelu(a_uv), 2, axis=-1)
    v = _layernorm(v, ln_w, ln_b)
    vc = v.reshape(bn, s // A_CHUNK, A_CHUNK, A_GROUPS, A_GROUP_DIM)
    mixed = jnp.einsum('gpq,bnqgd->bnpgd', w_sp, vc) + b_sp.T[:, :, None]
    return u * mixed.reshape(bn, s, A_WIDTH)


def _gated_delta_chunked(q, k, v, g, beta):
    bn, s, h, dk = q.shape
    dv = v.shape[-1]
    c = DN_CHUNK
    nc = s // c
    to_chunks = lambda t: t.reshape(bn, nc, c, h, -1).transpose(0, 3, 1, 2, 4)
    q, k, v = to_chunks(q), to_chunks(k), to_chunks(v)
    g = g.reshape(bn, nc, c, h).transpose(0, 3, 1, 2)
    beta = beta.reshape(bn, nc, c, h).transpose(0, 3, 1, 2)
    g = jnp.cumsum(g, axis=-1)
    lower_incl = jnp.tril(jnp.ones((c, c), dtype=bool))
    strict = jnp.tril(jnp.ones((c, c), dtype=bool), -1)
    diff = g[..., :, None] - g[..., None, :]
    decay = jnp.where(lower_incl, jnp.exp(jnp.where(lower_incl, diff, 0.0)), 0.0)
    k_beta = k * beta[..., None]
    low = jnp.where(strict, jnp.einsum('bhnid,bhnjd->bhnij', k_beta, k) * decay, 0.0)
    a_mat = low + jnp.eye(c, dtype=low.dtype)
    rhs = jnp.concatenate([v * beta[..., None], k_beta * jnp.exp(g)[..., None]], axis=-1)
    sol = lax.linalg.triangular_solve(a_mat, rhs, left_side=True, lower=True, unit_diagonal=True)
    u_c, w_c = sol[..., :dv], sol[..., dv:]
    attn = jnp.where(lower_incl, jnp.einsum('bhnid,bhnjd->bhnij', q, k) * decay, 0.0)
    q_dec = q * jnp.exp(g)[..., None]
    g_last = g[..., -1]
    k_dec = k * jnp.exp(g_last[..., None] - g)[..., None]
    lead = lambda t: jnp.moveaxis(t, 2, 0)
    xs = (lead(u_c), lead(w_c), lead(attn), lead(q_dec), lead(k_dec), lead(jnp.exp(g_last)))

    def step(state, inp):
        u_i, w_i, attn_i, qd_i, kd_i, gl_i = inp
        v_new = u_i - jnp.einsum('bhcd,bhde->bhce', w_i, state)
        o = jnp.einsum('bhcd,bhde->bhce', qd_i, state) + jnp.einsum('bhij,bhje->bhie', attn_i, v_new)
        state = state * gl_i[..., None, None] + jnp.einsum('bhcd,bhce->bhde', kd_i, v_new)
        return state, o

    state0 = jnp.zeros((bn, h, dk, dv), jnp.float32)
    _, o = lax.scan(step, state0, xs)
    return o.transpose(1, 0, 3, 2, 4).reshape(bn, s, h, dv)


def _bidir_gated_deltanet(qkv, z, beta_raw, alpha_raw, conv_w, a_log, dt_bias, o_norm_w):
    bn, s, _ = qkv.shape
    f32 = jnp.float32
    qkv = jax.nn.silu(_dwconv(qkv, conv_w, DN_CONV // 2))
    q, k, v = jnp.split(qkv, 3, axis=-1)
    heads = lambda t: t.reshape(bn, s, DN_HEADS, DN_HEAD_DIM).astype(f32)
    q = _l2norm(heads(q)) * (DN_HEAD_DIM ** -0.5)
    k = _l2norm(heads(k))
    v = heads(v)
    beta = jax.nn.sigmoid(beta_raw.astype(f32))
    g = -jnp.exp(a_log.astype(f32)).reshape(2 * DN_HEADS) * jax.nn.softplus(
        alpha_raw.astype(f32) + dt_bias.astype(f32).reshape(2 * DN_HEADS))
    flip = lambda t: jnp.flip(t, axis=1)
    o_fwd = _gated_delta_chunked(q, k, v, g[..., :DN_HEADS], beta[..., :DN_HEADS])
    o_bwd = flip(_gated_delta_chunked(flip(q), flip(k), flip(v), flip(g[..., DN_HEADS:]), flip(beta[..., DN_HEADS:])))
    o = _rmsnorm(o_fwd + o_bwd, o_norm_w) * jax.nn.silu(heads(z))
    return o.reshape(bn, s, DN_WIDTH).astype(z.dtype)


def _rope_tables(positions):
    inv_freq = ROPE_THETA ** (-jnp.arange(0, MLA_ROPE, 2, dtype=jnp.float32) / MLA_ROPE)
    ang = positions.astype(jnp.float32)[..., None] * inv_freq
    return jnp.cos(ang)[:, :, None, :], jnp.sin(ang)[:, :, None, :]


def _apply_rope(x, cos, sin):
    xf = x.astype(jnp.float32)
    x1, x2 = jnp.split(xf, 2, axis=-1)
    return jnp.concatenate([x1 * cos - x2 * sin, x2 * cos + x1 * sin], axis=-1).astype(x.dtype)


def _block_attention(q, k, v):
    bn, s, h, dq = q.shape
    dv = v.shape[-1]
    nb = s // Q_BLOCK
    qb = q.reshape(bn, nb, Q_BLOCK, h, dq).transpose(1, 0, 2, 3, 4)
    scale = dq ** -0.5

    def one(q_blk):
        sc = jnp.einsum('bqhd,bkhd->bhqk', q_blk, k, preferred_element_type=jnp.float32) * scale
        p = jax.nn.softmax(sc, axis=-1).astype(v.dtype)
        return jnp.einsum('bhqk,bkhe->bqhe', p, v)

    o = lax.map(one, qb)
    return o.transpose(1, 0, 2, 3, 4).reshape(bn, s, h, dv)


def _mla(cq, ckv, kr, positions, q_norm_w, kv_norm_w, w_uq, w_ukv):
    bn, s, _ = cq.shape
    q = (_rmsnorm(cq, q_norm_w) @ w_uq).reshape(bn, s, MLA_HEADS, MLA_NOPE + MLA_ROPE)
    kv = (_rmsnorm(ckv, kv_norm_w) @ w_ukv).reshape(bn, s, MLA_HEADS, MLA_NOPE + MLA_V)
    q_nope, q_pe = q[..., :MLA_NOPE], q[..., MLA_NOPE:]
    k_nope, v = kv[..., :MLA_NOPE], kv[..., MLA_NOPE:]
    cos, sin = _rope_tables(positions)
    q_pe = _apply_rope(q_pe, cos, sin)
    k_pe = _apply_rope(kr[:, :, None, :], cos, sin)
    q_full = jnp.concatenate([q_nope, q_pe], axis=-1)
    k_full = jnp.concatenate([k_nope, jnp.broadcast_to(k_pe, (bn, s, MLA_HEADS, MLA_ROPE))], axis=-1)
    return _block_attention(q_full, k_full, v).reshape(bn, s, MLA_WIDTH)


def _token_mixer(h, positions, w_in, b_gate, a_ln_w, a_ln_b, a_w_sp, a_b_sp, dn_conv_w, dn_a_log, dn_dt_bias,
                 dn_o_norm_w, mla_q_norm_w, mla_kv_norm_w, mla_w_uq, mla_w_ukv, w_branch, w_o):
    bn, s, _ = h.shape
    proj = h @ w_in
    a_uv, dn_qkv, dn_z, dn_beta, dn_alpha, mla_cq, mla_ckv, mla_kr, gate_raw = _split_sizes(proj, IN_SIZES)
    o_a = _spatial_gating(a_uv, a_ln_w, a_ln_b, a_w_sp, a_b_sp)
    o_b = _bidir_gated_deltanet(dn_qkv, dn_z, dn_beta, dn_alpha, dn_conv_w, dn_a_log, dn_dt_bias, dn_o_norm_w)
    o_c = _mla(mla_cq, mla_ckv, mla_kr, positions, mla_q_norm_w, mla_kv_norm_w, mla_w_uq, mla_w_ukv)
    branches = jnp.stack([o_a, o_b, o_c], axis=2)
    up = jnp.einsum('bsnw,nwd->bsnd', branches, w_branch)
    gates = jax.nn.sigmoid(gate_raw.reshape(bn, s, N_BRANCH, D_MODEL) + b_gate)
    return jnp.sum(gates * up, axis=2) @ w_o


def _conv_ffn(h, w_up, conv_w, conv_b, w_down):
    hu = _dwconv(h @ w_up, conv_w, FFN_CONV // 2) + conv_b
    a, b = jnp.split(hu, 2, axis=-1)
    return (jax.nn.silu(a) * b) @ w_down


def setup_inputs(seed: int = 0) -> dict:
    key = jax.random.key(seed)
    ks = jax.random.split(key, 32)
    f32 = jnp.float32
    nrm = lambda k, shape, scale: jax.random.normal(k, shape, f32) * scale
    x = nrm(ks[0], (BATCH, SEQ, D_MODEL), 1.0)
    c = nrm(ks[1], (BATCH, D_MODEL), 1.0)
    offset = jax.random.randint(ks[2], (BATCH, 1), 0, MAX_POS_OFFSET, dtype=jnp.int32)
    positions = offset + jnp.arange(SEQ, dtype=jnp.int32)[None, :]
    w_ada = nrm(ks[3], (DEPTH, D_MODEL, 6 * D_MODEL), D_MODEL ** -0.5)
    b_ada = nrm(ks[4], (DEPTH, 6 * D_MODEL), 0.01)
    norm_w = 1.0 + nrm(ks[5], (DEPTH, 4, D_MODEL), 0.05)
    w_in = nrm(ks[6], (DEPTH, D_MODEL, N_IN), D_MODEL ** -0.5)
    b_gate = nrm(ks[7], (DEPTH, N_BRANCH, D_MODEL), 0.01)
    a_ln_w = 1.0 + nrm(ks[8], (DEPTH, A_WIDTH), 0.05)
    a_ln_b = nrm(ks[9], (DEPTH, A_WIDTH), 0.01)
    a_w_sp = nrm(ks[10], (DEPTH, A_GROUPS, A_CHUNK, A_CHUNK), A_CHUNK ** -0.5)
    a_b_sp = 1.0 + nrm(ks[11], (DEPTH, A_GROUPS, A_CHUNK), 0.05)
    dn_conv_w = nrm(ks[12], (DEPTH, DN_CONV, 3 * DN_WIDTH), DN_CONV ** -0.5)
    dn_a_log = jnp.log(jax.random.uniform(ks[13], (DEPTH, 2, DN_HEADS), f32, 1.0, 16.0))
    dt = jnp.exp(jax.random.uniform(ks[14], (DEPTH, 2, DN_HEADS), f32, math.log(1e-3), math.log(1e-1)))
    dn_dt_bias = dt + jnp.log(-jnp.expm1(-dt))
    dn_o_norm_w = 1.0 + nrm(ks[15], (DEPTH, DN_HEAD_DIM), 0.05)
    mla_q_norm_w = 1.0 + nrm(ks[16], (DEPTH, MLA_Q_RANK), 0.05)
    mla_kv_norm_w = 1.0 + nrm(ks[17], (DEPTH, MLA_KV_RANK), 0.05)
    mla_w_uq = nrm(ks[18], (DEPTH, MLA_Q_RANK, MLA_HEADS * (MLA_NOPE + MLA_ROPE)), MLA_Q_RANK ** -0.5)
    mla_w_ukv = nrm(ks[19], (DEPTH, MLA_KV_RANK, MLA_HEADS * (MLA_NOPE + MLA_V)), MLA_KV_RANK ** -0.5)
    w_branch = nrm(ks[20], (DEPTH, N_BRANCH, BRANCH_WIDTH, D_MODEL), BRANCH_WIDTH ** -0.5)
    w_o = nrm(ks[21], (DEPTH, D_MODEL, D_MODEL), D_MODEL ** -0.5)
    ffn_w_up = nrm(ks[22], (DEPTH, D_MODEL, 2 * D_FF), D_MODEL ** -0.5)
    ffn_conv_w = nrm(ks[23], (DEPTH, FFN_CONV, 2 * D_FF), FFN_CONV ** -0.5)
    ffn_conv_b = nrm(ks[24], (DEPTH, 2 * D_FF), 0.01)
    ffn_w_down = nrm(ks[25], (DEPTH, D_FF, D_MODEL), D_FF ** -0.5)
    return {"x": x, "c": c, "positions": positions, "w_ada": w_ada, "b_ada": b_ada, "norm_w": norm_w,
            "w_in": w_in, "b_gate": b_gate, "a_ln_w": a_ln_w, "a_ln_b": a_ln_b, "a_w_sp": a_w_sp,
            "a_b_sp": a_b_sp, "dn_conv_w": dn_conv_w, "dn_a_log": dn_a_log, "dn_dt_bias": dn_dt_bias,
            "dn_o_norm_w": dn_o_norm_w, "mla_q_norm_w": mla_q_norm_w, "mla_kv_norm_w": mla_kv_norm_w,
            "mla_w_uq": mla_w_uq, "mla_w_ukv": mla_w_ukv, "w_branch": w_branch, "w_o": w_o,
            "ffn_w_up": ffn_w_up, "ffn_conv_w": ffn_conv_w, "ffn_conv_b": ffn_conv_b, "ffn_w_down": ffn_w_down}


def reference(x, c, positions, w_ada, b_ada, norm_w, w_in, b_gate, a_ln_w, a_ln_b, a_w_sp, a_b_sp, dn_conv_w,
              dn_a_log, dn_dt_bias, dn_o_norm_w, mla_q_norm_w, mla_kv_norm_w, mla_w_uq, mla_w_ukv, w_branch, w_o,
              ffn_w_up, ffn_conv_w, ffn_conv_b, ffn_w_down):
    c_act = jax.nn.silu(c)
    for l in range(DEPTH):
        mod = jnp.split(c_act @ w_ada[l] + b_ada[l], 6, axis=-1)
        sh1, sc1, gt1, sh2, sc2, gt2 = [m[:, None, :] for m in mod]
        h = _rmsnorm(x, norm_w[l, 0]) * (1.0 + sc1) + sh1
        y = _token_mixer(h, positions, w_in[l], b_gate[l], a_ln_w[l], a_ln_b[l], a_w_sp[l], a_b_sp[l],
                         dn_conv_w[l], dn_a_log[l], dn_dt_bias[l], dn_o_norm_w[l], mla_q_norm_w[l],
                         mla_kv_norm_w[l], mla_w_uq[l], mla_w_ukv[l], w_branch[l], w_o[l])
        x = x + gt1 * _rmsnorm(y, norm_w[l, 1])
        h = _rmsnorm(x, norm_w[l, 2]) * (1.0 + sc2) + sh2
        y = _conv_ffn(h, ffn_w_up[l], ffn_conv_w[l], ffn_conv_b[l], ffn_w_down[l])
        x = x + gt2 * _rmsnorm(y, norm_w[l, 3])
    return x
```

```python
import os
import numpy as np
import concourse.bass as bass
import concourse.mybir as mybir
from concourse.bass_utils import run_bass_kernel_spmd

F32 = mybir.dt.float32
BF16 = mybir.dt.bfloat16
U8 = mybir.dt.uint8
I32 = mybir.dt.int32
AF = mybir.ActivationFunctionType
ALU = mybir.AluOpType

P = 128
SEQ = 2048
D = 1024
TT = SEQ // P
KT = D // P
DEPTH = 4
PAD = 2
HTW = SEQ + 2 * PAD
N_IN = 6864
D_FF = 2816
NFC = D_FF // P
C_AUV, C_QKV, C_Z, C_BETA, C_ALPHA, C_CQ, C_CKV, C_KR, C_GATE = 0, 1024, 2560, 3072, 3080, 3088, 3472, 3728, 3792
CP_QNW, CP_KVNW, CP_DNCONV, CP_FCW, CP_FCB, CP_BGATE, CP_BSP = 0, 3, 5, 65, 197, 241, 265
NCOL = 269
RP_NW, RP_BADA, RP_LNW, RP_LNB, RP_ONW, RP_ALOG, RP_DTB = 0, 4096, 10240, 10752, 11264, 11392, 11400
NROW = 11408
RMS_EPS = 1e-6
LN_EPS = 1e-5


class Res:
    __slots__ = ("w", "r")

    def __init__(self):
        self.w = None
        self.r = {}


class Sched:
    def __init__(self, nc, ndma=8):
        self.nc = nc
        self.engs = {}
        for k, e in (("pe", nc.tensor), ("act", nc.scalar), ("dve", nc.vector), ("pool", nc.gpsimd), ("sp", nc.sync)):
            self.engs[k] = dict(eng=e, sem=nc.alloc_semaphore("s_" + k), cnt=0, seen={}, name=k)
        self.slots = {q: [dict(sem=nc.alloc_semaphore(f"d_{q}{i}"), tot=0) for i in range(ndma)] for q in ("sp", "pool")}
        self.rr = {"sp": 0, "pool": 0}
        self.nins = 0
        self.nwaits = 0

    def _wait(self, E, tok):
        sem, val = tok
        if E["seen"].get(sem.num, 0) >= val:
            return
        E["seen"][sem.num] = val
        E["eng"].wait_ge(sem, val)
        self.nwaits += 1

    def _deps(self, E, reads, writes, skip_self):
        toks = {}

        def add(tok):
            if tok is None:
                return
            sem, val = tok
            if skip_self and sem.num == E["sem"].num:
                return
            if sem.num not in toks or toks[sem.num][1] < val:
                toks[sem.num] = tok
        for r in reads:
            add(r.w)
        for w in writes:
            add(w.w)
            for t in w.r.values():
                add(t)
        for tok in toks.values():
            self._wait(E, tok)

    def _commit(self, tok, reads, writes):
        for r in reads:
            r.r[tok[0].num] = tok
        for w in writes:
            w.w = tok
            w.r = {}

    def op(self, ek, fn, reads=(), writes=(), inc=True):
        E = self.engs[ek]
        self._deps(E, reads, writes, skip_self=(ek == "pe"))
        ins = fn(E["eng"])
        self.nins += 1
        if inc:
            E["cnt"] += 1
            ins.then_inc(E["sem"], 1)
            tok = (E["sem"], E["cnt"])
        else:
            tok = (E["sem"], E["cnt"] + 1)
        self._commit(tok, reads, writes)
        return tok

    def dma(self, q, out, in_, reads=(), writes=()):
        E = self.engs[q]
        sl = self.slots[q][self.rr[q]]
        self.rr[q] = (self.rr[q] + 1) % len(self.slots[q])
        if sl["tot"] > 0:
            self._wait(E, (sl["sem"], sl["tot"]))
        self._deps(E, reads, writes, skip_self=False)
        ins = E["eng"].dma_start(out=out, in_=in_)
        sl["tot"] += 16
        ins.then_inc(sl["sem"], 16)
        tok = (sl["sem"], sl["tot"])
        self.nins += 1
        self._commit(tok, reads, writes)
        return tok

    def barrier(self):
        toks = []
        for F in self.engs.values():
            if F["cnt"] > 0:
                toks.append((F["sem"], F["cnt"]))
        for q in self.slots:
            for sl in self.slots[q]:
                if sl["tot"] > 0:
                    toks.append((sl["sem"], sl["tot"]))
        for E in self.engs.values():
            for t in toks:
                if t[0].num != E["sem"].num:
                    self._wait(E, t)

    def finish(self):
        E = self.engs["sp"]
        for q in self.slots:
            for sl in self.slots[q]:
                if sl["tot"] > 0:
                    self._wait(E, (sl["sem"], sl["tot"]))


class Arena:
    def __init__(self, nc, nbytes):
        self.t = nc.alloc_sbuf_tensor("arena", [P, nbytes], U8)
        self.off = 0
        self.nbytes = nbytes

    def alloc(self, shape, dt, parts=P):
        sz = 4 if dt in (F32, I32) else 2
        n = int(np.prod(shape)) * sz
        assert self.off + n <= self.nbytes, ("SBUF arena overflow", self.off, n)
        a = self.t[0:parts, self.off:self.off + n].bitcast(dt)
        self.off += (n + 63) // 64 * 64
        if len(shape) == 2:
            a = a.rearrange("p (a b) -> p a b", a=shape[0])
        elif len(shape) == 3:
            a = a.rearrange("p (a b c) -> p a b c", a=shape[0], b=shape[1])
        elif len(shape) == 4:
            a = a.rearrange("p (a b c d) -> p a b c d", a=shape[0], b=shape[1], c=shape[2])
        return a

    def mark(self):
        return self.off

    def release(self, m):
        self.off = m


def build(n_layers=DEPTH, parts=("A", "B", "C")):
    nc = bass.Bass("TRN2", target_bir_lowering=False)
    L = DEPTH
    dt_in = lambda name, shape, dt=F32: nc.dram_tensor(name, shape, dt, kind="ExternalInput").ap()
    x_in = dt_in("x", [SEQ, D])
    cT_d = dt_in("cT", [P, KT])
    pos_d = dt_in("pos", [1, SEQ], I32)
    w_ada = dt_in("w_ada", [L, D, 6 * D])
    w_in = dt_in("w_in", [L, D, N_IN])
    w_uq = dt_in("w_uq", [L, 384, 768])
    w_uqsw = dt_in("w_uqsw", [L, 384, 256])
    w_krsw = dt_in("w_krsw", [L, D, 64])
    w_ukv = dt_in("w_ukv", [L, 256, 1024])
    w_branch = dt_in("w_branch", [L, 3, 512, D])
    w_o = dt_in("w_o", [L, D, D])
    w_up = dt_in("w_up", [L, D, 2 * D_FF])
    w_down = dt_in("w_down", [L, D_FF, D])
    w_spT = dt_in("w_spT", [L, 4, P, P])
    colp_d = dt_in("colp", [L, P, NCOL])
    rowp_d = dt_in("rowp", [L, 1, NROW])
    cst_d = dt_in("cst", [P, 388])
    out = nc.dram_tensor("out", [SEQ, D], F32, kind="ExternalOutput").ap()

    S = Sched(nc)
    DBG = os.environ.get("KDBG") == "1"
    dbg_t = nc.dram_tensor("dbg", [24, P, 512], F32, kind="ExternalOutput").ap() if DBG else None
    dbg_i = [0]

    def dbg(name, ap, r, width):
        if not DBG or dbg_i[0] >= 24:
            return
        i = dbg_i[0]
        dbg_i[0] += 1
        print("DBGSLOT", i, name, width)
        S.dma("pool", dbg_t[i, :, 0:width], ap, reads=[r], writes=[Res()])
    ar = Arena(nc, 207 * 1024)
    psum = nc.alloc_psum_tensor("ps", [P, 8 * 512], F32)
    PS = [psum[:, b * 512:(b + 1) * 512] for b in range(8)]
    PSB = [PS[b].bitcast(BF16) for b in range(8)]
    r_ps = [Res() for _ in range(8)]

    op = S.op

    identf = ar.alloc([P], F32)
    identb = ar.alloc([P], BF16)
    onesb = ar.alloc([P], BF16)
    cst = ar.alloc([388], F32)
    r_const = Res()
    op("pool", lambda e: e.memset(identf, 0.0), writes=[r_const])
    op("pool", lambda e: e.affine_select(out=identf, in_=identf, pattern=[[-1, P]], compare_op=ALU.not_equal,
                                         fill=1.0, base=0, channel_multiplier=1), reads=[r_const], writes=[r_const])
    op("dve", lambda e: e.tensor_copy(out=identb, in_=identf), reads=[r_const], writes=[r_const])
    op("dve", lambda e: e.memset(onesb, 1.0), writes=[r_const])
    S.dma("sp", cst, cst_d, writes=[r_const])

    hT_off = []
    hT = []
    for _ in range(2):
        hT_off.append(ar.off)
        hT.append(ar.alloc([KT, HTW], BF16))
    r_hT = [Res(), Res()]
    for i in range(2):
        op("pool", lambda e, i=i: e.memset(hT[i], 0.0), writes=[r_hT[i]])
    mod = ar.alloc([6, D], F32)
    r_mod = Res()
    cbf = ar.alloc([KT, P], BF16)
    r_cbf = Res()
    colp = ar.alloc([NCOL], F32)
    r_colp = Res()
    base_mark = ar.mark()

    m0 = ar.mark()
    ctile = ar.alloc([KT], F32)
    r_ct = Res()
    S.dma("sp", ctile, cT_d, writes=[r_ct])
    op("act", lambda e: e.activation(out=ctile, in_=ctile, func=AF.Silu), reads=[r_ct], writes=[r_ct])
    op("dve", lambda e: e.tensor_copy(out=cbf, in_=ctile.unsqueeze(2).to_broadcast([P, KT, P])), reads=[r_ct], writes=[r_cbf])
    S.barrier()
    ar.release(m0)

    def wload(dst, src, r):
        return S.dma("pool", dst, src, writes=[r])

    def phase_mod(l):
        m = ar.mark()
        S.dma("sp", colp, colp_d[l], writes=[r_colp])
        wb = [ar.alloc([KT, 512], BF16) for _ in range(2)]
        r_wb = [Res(), Res()]
        bb = [ar.alloc([512], F32) for _ in range(2)]
        r_bb = [Res(), Res()]
        nwb = ar.alloc([4 * D], F32)
        r_nwb = Res()
        S.dma("sp", nwb, rowp_d[l, :, RP_NW:RP_NW + 4 * D].partition_broadcast(P), writes=[r_nwb])
        modf = mod.rearrange("p a b -> p (a b)")
        wsrc = w_ada[l].rearrange("(kt p) n -> p kt n", p=P)
        for n in range(12):
            b = n % 2
            wload(wb[b], wsrc[:, :, n * 512:(n + 1) * 512], r_wb[b])
            S.dma("sp", bb[b], rowp_d[l, :, RP_BADA + n * 512:RP_BADA + (n + 1) * 512].partition_broadcast(P), writes=[r_bb[b]])
            pb = n % 2
            for kt in range(KT):
                op("pe", lambda e, kt=kt, b=b, pb=pb: e.matmul(PS[pb], lhsT=cbf[:, kt, :], rhs=wb[b][:, kt, :], start=(kt == 0), stop=(kt == KT - 1)),
                   reads=[r_cbf, r_wb[b]], writes=[r_ps[pb]], inc=(kt == KT - 1))
            op("dve", lambda e, n=n, b=b, pb=pb: e.tensor_tensor(out=modf[:, n * 512:(n + 1) * 512], in0=PS[pb], in1=bb[b], op=ALU.add),
               reads=[r_ps[pb], r_bb[b]], writes=[r_mod])
        nw = nwb.rearrange("p (a b) -> p a b", a=4)
        op("dve", lambda e: e.scalar_tensor_tensor(out=mod[:, 1, :], in0=mod[:, 1, :], scalar=1.0, in1=nw[:, 0, :], op0=ALU.add, op1=ALU.mult),
           reads=[r_nwb, r_mod], writes=[r_mod])
        op("dve", lambda e: e.tensor_tensor(out=mod[:, 2, :], in0=mod[:, 2, :], in1=nw[:, 1, :], op=ALU.mult), reads=[r_nwb, r_mod], writes=[r_mod])
        op("dve", lambda e: e.scalar_tensor_tensor(out=mod[:, 4, :], in0=mod[:, 4, :], scalar=1.0, in1=nw[:, 2, :], op0=ALU.add, op1=ALU.mult),
           reads=[r_nwb, r_mod], writes=[r_mod])
        op("dve", lambda e: e.tensor_tensor(out=mod[:, 5, :], in0=mod[:, 5, :], in1=nw[:, 3, :], op=ALU.mult), reads=[r_nwb, r_mod], writes=[r_mod])
        S.barrier()
        ar.release(m)

    def make_norm_ctx():
        ctx = dict(
            ss=ar.alloc([4], F32), r_ss=Res(),
            junk=ar.alloc([D], BF16), r_junk=Res(),
            hf=ar.alloc([D], F32), r_hf=Res(),
            hb=[ar.alloc([D], BF16) for _ in range(2)], r_hb=[Res(), Res()], i=0)
        return ctx

    def norm_tile(ctx, xt, r_xt, A, sh, hdst, r_hdst, tt, psb):
        ss, r_ss = ctx["ss"], ctx["r_ss"]
        k = ctx["i"] % 2
        ctx["i"] += 1
        hb, r_hb = ctx["hb"][k], ctx["r_hb"][k]
        op("act", lambda e: e.activation(out=ctx["junk"], in_=xt, func=AF.Square, accum_out=ss[:, 0:1]), reads=[r_xt], writes=[ctx["r_junk"], r_ss])
        op("dve", lambda e: e.tensor_scalar(out=ss[:, 1:2], in0=ss[:, 0:1], scalar1=1.0 / D, scalar2=RMS_EPS, op0=ALU.mult, op1=ALU.add), reads=[r_ss], writes=[r_ss])
        op("act", lambda e: e.activation(out=ss[:, 2:3], in_=ss[:, 1:2], func=AF.Sqrt), reads=[r_ss], writes=[r_ss])
        op("dve", lambda e: e.reciprocal(out=ss[:, 3:4], in_=ss[:, 2:3]), reads=[r_ss], writes=[r_ss])
        op("dve", lambda e: e.scalar_tensor_tensor(out=ctx["hf"], in0=xt, scalar=ss[:, 3:4], in1=A, op0=ALU.mult, op1=ALU.mult),
           reads=[r_xt, r_ss, r_mod], writes=[ctx["r_hf"]])
        op("dve", lambda e: e.tensor_tensor(out=hb, in0=ctx["hf"], in1=sh, op=ALU.add), reads=[ctx["r_hf"], r_mod], writes=[r_hb])
        for kt in range(KT):
            op("pe", lambda e, kt=kt: e.transpose(PSB[psb][:, kt * P:(kt + 1) * P], hb[:, kt * P:(kt + 1) * P], identb),
               reads=[r_hb, r_const], writes=[r_ps[psb]], inc=(kt == KT - 1))
        op("act", lambda e: e.activation(out=hdst[:, :, PAD + tt * P:PAD + (tt + 1) * P], in_=PSB[psb].rearrange("p (a b) -> p a b", a=KT), func=AF.Copy),
           reads=[r_ps[psb]], writes=[r_hdst])

    def phase_norm_from_dram(src, A, sh, hdst, r_hdst):
        m = ar.mark()
        ctx = make_norm_ctx()
        xt = [ar.alloc([D], F32) for _ in range(2)]
        r_xt = [Res(), Res()]
        for tt in range(TT):
            b = tt % 2
            S.dma("sp", xt[b], src[tt * P:(tt + 1) * P, :], writes=[r_xt[b]])
            norm_tile(ctx, xt[b], r_xt[b], A, sh, hdst, r_hdst, tt, 6 + b)
        S.barrier()
        ar.release(m)

    def make_resid_ctx():
        return dict(xt=[ar.alloc([D], F32) for _ in range(2)], r_xt=[Res(), Res()],
                    ss=ar.alloc([8], F32), r_ss=Res(), junk=ar.alloc([D], BF16), r_junk=Res(),
                    tmp=ar.alloc([D], F32), r_tmp=Res(), i=0)

    def resid_tile(rc, pbanks, G, src, tt, r_out_tiles, nctx=None, nA=None, nsh=None, hdst=None, r_hdst=None, npsb=None):
        k = rc["i"] % 2
        rc["i"] += 1
        xt, r_xt = rc["xt"][k], rc["r_xt"][k]
        ss, r_ss = rc["ss"], rc["r_ss"]
        S.dma("sp", xt, src[tt * P:(tt + 1) * P, :], reads=[r_out_tiles[tt]] if src is out else [], writes=[r_xt])
        for j, pb in enumerate(pbanks):
            op("act", lambda e, j=j, pb=pb: e.activation(out=rc["junk"][:, j * 512:(j + 1) * 512], in_=PS[pb], func=AF.Square, accum_out=ss[:, j:j + 1]),
               reads=[r_ps[pb]], writes=[rc["r_junk"], r_ss])
        op("dve", lambda e: e.tensor_tensor(out=ss[:, 2:3], in0=ss[:, 0:1], in1=ss[:, 1:2], op=ALU.add), reads=[r_ss], writes=[r_ss])
        op("dve", lambda e: e.tensor_scalar(out=ss[:, 3:4], in0=ss[:, 2:3], scalar1=1.0 / D, scalar2=RMS_EPS, op0=ALU.mult, op1=ALU.add), reads=[r_ss], writes=[r_ss])
        op("act", lambda e: e.activation(out=ss[:, 4:5], in_=ss[:, 3:4], func=AF.Sqrt), reads=[r_ss], writes=[r_ss])
        op("dve", lambda e: e.reciprocal(out=ss[:, 5:6], in_=ss[:, 4:5]), reads=[r_ss], writes=[r_ss])
        for j, pb in enumerate(pbanks):
            op("dve", lambda e, j=j, pb=pb: e.scalar_tensor_tensor(out=rc["tmp"][:, j * 512:(j + 1) * 512], in0=PS[pb], scalar=ss[:, 5:6],
                                                                 in1=G[:, j * 512:(j + 1) * 512], op0=ALU.mult, op1=ALU.mult),
               reads=[r_ps[pb], r_ss, r_mod], writes=[rc["r_tmp"]])
        op("dve", lambda e: e.tensor_tensor(out=xt, in0=xt, in1=rc["tmp"], op=ALU.add), reads=[rc["r_tmp"], r_xt], writes=[r_xt])
        S.dma("sp", out[tt * P:(tt + 1) * P, :], xt, reads=[r_xt], writes=[r_out_tiles[tt]])
        if nctx is not None:
            norm_tile(nctx, xt, r_xt, nA, nsh, hdst, r_hdst, tt, npsb)

    def phase_ffn(l, hsrc, r_hsrc, r_out_tiles, last):
        m = ar.mark()
        TC = 512
        gT = ar.alloc([NFC, TC], BF16)
        r_gT = [Res() for _ in range(NFC)]
        wu = [ar.alloc([KT, 2, P], BF16) for _ in range(2)]
        r_wu = [Res(), Res()]
        wd = [ar.alloc([NFC, 512], BF16) for _ in range(2)]
        r_wd = [Res(), Res()]
        ca = [ar.alloc([TC], F32) for _ in range(2)]
        r_ca = [Res(), Res()]
        sa = ar.alloc([TC], F32)
        r_sa = Res()
        rc = make_resid_ctx()
        nctx = None if last else make_norm_ctx()
        fcw = colp[:, CP_FCW:CP_FCW + 132].rearrange("p (a b) -> p a b", a=44)
        fcb = colp[:, CP_FCB:CP_FCB + 44]
        wup_src = w_up[l].rearrange("(kt p) n -> p kt n", p=P)
        wdn_src = w_down[l].rearrange("(fc p) n -> p fc n", p=P)
        pwin = [psum[:, 0:1024], psum[:, 1024:2048], psum[:, 2048:3072]]
        r_pwin = [Res(), Res(), Res()]
        wi = 0
        pwi = 0
        for tc in range(SEQ // TC):
            c0 = tc * TC
            col0 = PAD + c0 - 1
            for fc in range(NFC):
                b = wi % 2
                wi += 1
                wload(wu[b][:, :, 0, :], wup_src[:, :, fc * P:(fc + 1) * P], r_wu[b])
                wload(wu[b][:, :, 1, :], wup_src[:, :, D_FF + fc * P:D_FF + (fc + 1) * P], r_wu[b])
                for ab in range(2):
                    pw, r_pw = pwin[pwi % 3], r_pwin[pwi % 3]
                    pwi += 1
                    cidx = fc + 22 * ab
                    wins = [(0, 512), (512, 2)]
                    for wi_, (o, n) in enumerate(wins):
                        for kt in range(KT):
                            op("pe", lambda e, kt=kt, o=o, n=n, ab=ab, b=b, pw=pw: e.matmul(pw[:, o:o + n], lhsT=wu[b][:, kt, ab, :], rhs=hsrc[:, kt, col0 + o:col0 + o + n],
                                                                                      start=(kt == 0), stop=(kt == KT - 1)),
                               reads=[r_wu[b], r_hsrc], writes=[r_pw], inc=(kt == KT - 1 and wi_ == 1))
                    cab = ca[ab]
                    op("act", lambda e, pw=pw, cab=cab, cidx=cidx: e.activation(out=cab, in_=pw[:, 1:1 + TC], func=AF.Identity, scale=fcw[:, cidx, 1:2], bias=fcb[:, cidx:cidx + 1]),
                       reads=[r_pw, r_colp], writes=[r_ca[ab]])
                    op("dve", lambda e, pw=pw, cab=cab, cidx=cidx: e.scalar_tensor_tensor(out=cab, in0=pw[:, 0:TC], scalar=fcw[:, cidx, 0:1], in1=cab, op0=ALU.mult, op1=ALU.add),
                       reads=[r_pw, r_colp, r_ca[ab]], writes=[r_ca[ab]])
                    op("dve", lambda e, pw=pw, cab=cab, cidx=cidx: e.scalar_tensor_tensor(out=cab, in0=pw[:, 2:2 + TC], scalar=fcw[:, cidx, 2:3], in1=cab, op0=ALU.mult, op1=ALU.add),
                       reads=[r_pw, r_colp, r_ca[ab]], writes=[r_ca[ab]])
                op("act", lambda e: e.activation(out=sa, in_=ca[0], func=AF.Silu), reads=[r_ca[0]], writes=[r_sa])
                op("dve", lambda e, fc=fc: e.tensor_tensor(out=gT[:, fc, :], in0=sa, in1=ca[1], op=ALU.mult), reads=[r_sa, r_ca[1]], writes=[r_gT[fc]])
            for half in range(2):
                wload(wd[half], wdn_src[:, :, half * 512:(half + 1) * 512], r_wd[half])
            for t8 in range(TC // P):
                tt = tc * (TC // P) + t8
                for half in range(2):
                    pb = 6 + half
                    for fc in range(NFC):
                        op("pe", lambda e, fc=fc, half=half, pb=pb, t8=t8: e.matmul(PS[pb], lhsT=gT[:, fc, t8 * P:(t8 + 1) * P], rhs=wd[half][:, fc, :],
                                                                                  start=(fc == 0), stop=(fc == NFC - 1)),
                           reads=[r_gT[fc], r_wd[half]], writes=[r_ps[pb]], inc=(fc == NFC - 1))
                if last:
                    resid_tile(rc, [6, 7], mod[:, 5, :], out, tt, r_out_tiles)
                else:
                    resid_tile(rc, [6, 7], mod[:, 5, :], out, tt, r_out_tiles, nctx, mod_next[:, 1, :], mod_next[:, 0, :], hT[0], r_hT[0], 6)
        S.barrier()
        ar.release(m)

    r_out_tiles = [Res() for _ in range(TT)]
    mod_next = mod
    obr = nc.dram_tensor("obr", [3, 512, SEQ], BF16, kind="Internal").ap()
    r_obr = [Res() for _ in range(3)]
    win_src = lambda l: w_in[l].rearrange("(kt p) n -> p kt n", p=P)

    r_rope = Res()
    rope_holder = {}

    def build_rope():
        ropeC = ar.alloc([SEQ], F32)
        ropeS = ar.alloc([SEQ], F32)
        rope_holder["C"] = ropeC
        rope_holder["S"] = ropeS
        m = ar.mark()
        posi = ar.alloc([SEQ], I32)
        ang = ar.alloc([SEQ], F32)
        kf = ar.alloc([SEQ], F32)
        ki = ar.alloc([SEQ], I32)
        msk = ar.alloc([SEQ], F32)
        r_t = Res()
        TWO_PI = 6.283185307179586
        C1 = 6.28125
        C2 = TWO_PI - C1
        S.dma("sp", posi, pos_d.partition_broadcast(P), writes=[r_t])
        op("dve", lambda e: e.tensor_copy(out=ang, in_=posi), reads=[r_t], writes=[r_t])
        op("dve", lambda e: e.tensor_scalar(out=ang, in0=ang, scalar1=cst[:, 0:1], scalar2=None, op0=ALU.mult), reads=[r_t, r_const], writes=[r_t])

        def reduce_sin(dst, shift):
            op("dve", lambda e: e.tensor_scalar(out=kf, in0=ang, scalar1=shift, scalar2=1.0 / TWO_PI, op0=ALU.add, op1=ALU.mult), reads=[r_t], writes=[r_t])
            op("dve", lambda e: e.tensor_copy(out=ki, in_=kf), reads=[r_t], writes=[r_t])
            op("dve", lambda e: e.tensor_copy(out=kf, in_=ki), reads=[r_t], writes=[r_t])
            op("dve", lambda e: e.scalar_tensor_tensor(out=dst, in0=kf, scalar=-C1, in1=ang, op0=ALU.mult, op1=ALU.add), reads=[r_t], writes=[r_t])
            op("dve", lambda e: e.scalar_tensor_tensor(out=dst, in0=kf, scalar=-C2, in1=dst, op0=ALU.mult, op1=ALU.add), reads=[r_t], writes=[r_t])
            if shift != 0.0:
                op("dve", lambda e: e.tensor_scalar(out=dst, in0=dst, scalar1=shift, scalar2=None, op0=ALU.add), reads=[r_t], writes=[r_t])
            op("dve", lambda e: e.tensor_scalar(out=msk, in0=dst, scalar1=3.141592653589793, scalar2=-TWO_PI, op0=ALU.is_gt, op1=ALU.mult), reads=[r_t], writes=[r_t])
            op("dve", lambda e: e.tensor_tensor(out=dst, in0=dst, in1=msk, op=ALU.add), reads=[r_t], writes=[r_t])
            op("dve", lambda e: e.tensor_scalar(out=msk, in0=dst, scalar1=-3.141592653589793, scalar2=TWO_PI, op0=ALU.is_lt, op1=ALU.mult), reads=[r_t], writes=[r_t])
            op("dve", lambda e: e.tensor_tensor(out=dst, in0=dst, in1=msk, op=ALU.add), reads=[r_t], writes=[r_t])
            op("dve", lambda e: e.tensor_scalar(out=dst, in0=dst, scalar1=3.1415925, scalar2=-3.1415925, op0=ALU.min, op1=ALU.max), reads=[r_t], writes=[r_t])
            op("act", lambda e: e.activation(out=dst, in_=dst, func=AF.Sin), reads=[r_t], writes=[r_t])
        reduce_sin(ropeS, 0.0)
        reduce_sin(ropeC, 1.5707963267948966)
        op("dve", lambda e: e.tensor_scalar(out=ropeS, in0=ropeS, scalar1=cst[:, 1:2], scalar2=None, op0=ALU.mult), reads=[r_t, r_const], writes=[r_rope])
        S.barrier()
        ar.release(m)

    def phase_A(l):
        m = ar.mark()
        wa = ar.alloc([KT, 1024], BF16)
        r_wa = Res()
        wload(wa[:, :, 0:512], win_src(l)[:, :, C_AUV:C_AUV + 512], r_wa)
        wload(wa[:, :, 512:1024], win_src(l)[:, :, C_AUV + 512:C_AUV + 1024], r_wa)
        wsp_f = ar.alloc([4, P], F32)
        wsp = ar.alloc([4, P], BF16)
        r_wsp = Res()
        S.dma("sp", wsp_f, w_spT[l].rearrange("g q p -> q g p"), writes=[r_wsp])
        op("dve", lambda e: e.tensor_copy(out=wsp, in_=wsp_f), reads=[r_wsp], writes=[r_wsp])
        lnw = ar.alloc([1024], F32)
        r_ln = Res()
        S.dma("sp", lnw, rowp_d[l, :, RP_LNW:RP_LNW + 1024].partition_broadcast(P), writes=[r_ln])
        bsp = colp[:, CP_BSP:CP_BSP + 4]
        ug = [ar.alloc([512], F32) for _ in range(2)]
        vg = [ar.alloc([512], F32) for _ in range(2)]
        vn = [ar.alloc([512], BF16) for _ in range(2)]
        oa = [ar.alloc([512], BF16) for _ in range(2)]
        oaT = [ar.alloc([4, P], BF16) for _ in range(2)]
        st = [ar.alloc([8], F32) for _ in range(2)]
        junk = ar.alloc([512], BF16)
        r_b = [[Res() for _ in range(7)] for _ in range(2)]
        r_junk = Res()
        dstv = obr[0].rearrange("(wc p) t -> p wc t", p=P)
        for tt in range(TT):
            b = tt % 2
            r_ug, r_vg, r_vn, r_oa, r_oaT, r_st, _ = r_b[b]
            c0 = PAD + tt * P
            for hf_ in range(2):
                pb = 0 + hf_ + 2 * b
                for kt in range(KT):
                    op("pe", lambda e, kt=kt, hf_=hf_, pb=pb: e.matmul(PS[pb], lhsT=hT[0][:, kt, c0:c0 + P], rhs=wa[:, kt, hf_ * 512:(hf_ + 1) * 512],
                                                                  start=(kt == 0), stop=(kt == KT - 1)),
                       reads=[r_hT[0], r_wa], writes=[r_ps[pb]], inc=(kt == KT - 1))
            pu, pv = 0 + 2 * b, 1 + 2 * b
            op("act", lambda e: e.activation(out=ug[b], in_=PS[pu], func=AF.Gelu_apprx_tanh), reads=[r_ps[pu]], writes=[r_ug])
            op("act", lambda e: e.activation(out=vg[b], in_=PS[pv], func=AF.Gelu_apprx_tanh, accum_out=st[b][:, 0:1]), reads=[r_ps[pv]], writes=[r_vg, r_st])
            op("act", lambda e: e.activation(out=junk, in_=vg[b], func=AF.Square, accum_out=st[b][:, 1:2]), reads=[r_vg], writes=[r_junk, r_st])
            op("dve", lambda e: e.tensor_scalar(out=st[b][:, 2:3], in0=st[b][:, 0:1], scalar1=1.0 / 512, scalar2=None, op0=ALU.mult), reads=[r_st], writes=[r_st])
            op("dve", lambda e: e.tensor_tensor(out=st[b][:, 3:4], in0=st[b][:, 2:3], in1=st[b][:, 2:3], op=ALU.mult), reads=[r_st], writes=[r_st])
            op("dve", lambda e: e.scalar_tensor_tensor(out=st[b][:, 4:5], in0=st[b][:, 1:2], scalar=1.0 / 512, in1=st[b][:, 3:4], op0=ALU.mult, op1=ALU.subtract), reads=[r_st], writes=[r_st])
            op("dve", lambda e: e.tensor_scalar(out=st[b][:, 4:5], in0=st[b][:, 4:5], scalar1=LN_EPS, scalar2=None, op0=ALU.add), reads=[r_st], writes=[r_st])
            op("act", lambda e: e.activation(out=st[b][:, 5:6], in_=st[b][:, 4:5], func=AF.Sqrt), reads=[r_st], writes=[r_st])
            op("dve", lambda e: e.reciprocal(out=st[b][:, 6:7], in_=st[b][:, 5:6]), reads=[r_st], writes=[r_st])
            op("dve", lambda e: e.tensor_scalar(out=vg[b], in0=vg[b], scalar1=st[b][:, 2:3], scalar2=st[b][:, 6:7], op0=ALU.subtract, op1=ALU.mult), reads=[r_st, r_vg], writes=[r_vg])
            op("dve", lambda e: e.tensor_tensor(out=vg[b], in0=vg[b], in1=lnw[:, 0:512], op=ALU.mult), reads=[r_vg, r_ln], writes=[r_vg])
            op("dve", lambda e: e.tensor_tensor(out=vn[b], in0=vg[b], in1=lnw[:, 512:1024], op=ALU.add), reads=[r_vg, r_ln], writes=[r_vn])
            pm = 4 + b
            for g in range(4):
                op("pe", lambda e, g=g: e.matmul(PS[pm][:, g * P:(g + 1) * P], lhsT=wsp[:, g, :], rhs=vn[b][:, g * P:(g + 1) * P], start=True, stop=True),
                   reads=[r_wsp, r_vn], writes=[r_ps[pm]], inc=(g == 3))
            for g in range(4):
                op("dve", lambda e, g=g: e.scalar_tensor_tensor(out=oa[b][:, g * P:(g + 1) * P], in0=PS[pm][:, g * P:(g + 1) * P], scalar=bsp[:, g:g + 1],
                                                              in1=ug[b][:, g * P:(g + 1) * P], op0=ALU.add, op1=ALU.mult),
                   reads=[r_ps[pm], r_colp, r_ug], writes=[r_oa])
            pt = 6 + b
            for g in range(4):
                op("pe", lambda e, g=g: e.transpose(PSB[pt][:, g * P:(g + 1) * P], oa[b][:, g * P:(g + 1) * P], identb),
                   reads=[r_oa, r_const], writes=[r_ps[pt]], inc=(g == 3))
            op("act", lambda e: e.activation(out=oaT[b], in_=PSB[pt][:, 0:512].rearrange("p (a b) -> p a b", a=4), func=AF.Copy), reads=[r_ps[pt]], writes=[r_oaT])
            S.dma("sp", dstv[:, :, tt * P:(tt + 1) * P], oaT[b], reads=[r_oaT], writes=[r_obr[0]])
        S.barrier()
        ar.release(m)

    def zero_branch(n):
        m = ar.mark()
        z = ar.alloc([4, 512], BF16)
        r_z = Res()
        op("dve", lambda e: e.memset(z, 0.0), writes=[r_z])
        dstv = obr[n].rearrange("(wc p) t -> p wc t", p=P)
        for c in range(4):
            S.dma("sp", dstv[:, :, c * 512:(c + 1) * 512], z, reads=[r_z], writes=[r_obr[n]])
        S.barrier()
        ar.release(m)

    def phase_merge(l, src):
        m = ar.mark()
        wo = ar.alloc([KT, D], BF16)
        r_wo = Res()
        for hf_ in range(2):
            wload(wo[:, :, hf_ * 512:(hf_ + 1) * 512], w_o[l].rearrange("(kt p) n -> p kt n", p=P)[:, :, hf_ * 512:(hf_ + 1) * 512], r_wo)
        oT = ar.alloc([3, 4, 512], BF16)
        r_oT = Res()
        wg = [ar.alloc([KT, 3, P], BF16) for _ in range(2)]
        r_wg = [Res(), Res()]
        wbr = [ar.alloc([3, 4, P], BF16) for _ in range(2)]
        r_wbr = [Res(), Res()]
        mT = ar.alloc([KT, 512], BF16)
        r_mT = Res()
        sg = [ar.alloc([512], F32) for _ in range(2)]
        r_sg = [Res(), Res()]
        macc = ar.alloc([512], F32)
        r_macc = Res()
        tmp = ar.alloc([512], F32)
        r_tmp = Res()
        rc = make_resid_ctx()
        nctx = make_norm_ctx()
        bg = colp[:, CP_BGATE:CP_BGATE + 24]
        osrc = obr.rearrange("n (wc p) t -> p n wc t", p=P)
        wi = 0
        gi = 0
        for tc in range(4):
            c0 = PAD + tc * 512
            for n in range(3):
                S.dma("sp", oT[:, n], osrc[:, n, :, tc * 512:(tc + 1) * 512], reads=[r_obr[n]], writes=[r_oT])
            for dc in range(KT):
                b = wi % 2
                wi += 1
                for n in range(3):
                    wload(wg[b][:, :, n, :], win_src(l)[:, :, C_GATE + n * D + dc * P:C_GATE + n * D + (dc + 1) * P], r_wg[b])
                    wload(wbr[b][:, n, :, :], w_branch[l, n].rearrange("(wc p) d -> p wc d", p=P)[:, :, dc * P:(dc + 1) * P], r_wbr[b])
                for n in range(3):
                    k = gi % 2
                    gi += 1
                    pg, pu = 0 + 2 * k, 1 + 2 * k
                    for kt in range(KT):
                        op("pe", lambda e, kt=kt, n=n, pg=pg: e.matmul(PS[pg], lhsT=wg[b][:, kt, n, :], rhs=hT[0][:, kt, c0:c0 + 512], start=(kt == 0), stop=(kt == KT - 1)),
                           reads=[r_wg[b], r_hT[0]], writes=[r_ps[pg]], inc=(kt == KT - 1))
                    for wc in range(4):
                        op("pe", lambda e, wc=wc, n=n, pu=pu: e.matmul(PS[pu], lhsT=wbr[b][:, n, wc, :], rhs=oT[:, n, wc, :], start=(wc == 0), stop=(wc == 3)),
                           reads=[r_wbr[b], r_oT], writes=[r_ps[pu]], inc=(wc == 3))
                    op("act", lambda e, n=n, k=k, pg=pg: e.activation(out=sg[k], in_=PS[pg], func=AF.Sigmoid, bias=bg[:, n * 8 + dc:n * 8 + dc + 1]),
                       reads=[r_ps[pg], r_colp], writes=[r_sg[k]])
                    if n == 0:
                        op("dve", lambda e, k=k, pu=pu: e.tensor_tensor(out=macc, in0=PS[pu], in1=sg[k], op=ALU.mult), reads=[r_ps[pu], r_sg[k]], writes=[r_macc])
                    else:
                        op("dve", lambda e, k=k, pu=pu: e.tensor_tensor(out=tmp, in0=PS[pu], in1=sg[k], op=ALU.mult), reads=[r_ps[pu], r_sg[k]], writes=[r_tmp])
                        if n == 1:
                            op("dve", lambda e: e.tensor_tensor(out=macc, in0=macc, in1=tmp, op=ALU.add), reads=[r_tmp, r_macc], writes=[r_macc])
                        else:
                            op("dve", lambda e, dc=dc: e.tensor_tensor(out=mT[:, dc, :], in0=macc, in1=tmp, op=ALU.add), reads=[r_tmp, r_macc], writes=[r_mT])
            for t4 in range(4):
                tt = tc * 4 + t4
                for hf_ in range(2):
                    pb = 4 + hf_
                    for dc in range(KT):
                        op("pe", lambda e, dc=dc, hf_=hf_, pb=pb, t4=t4: e.matmul(PS[pb], lhsT=mT[:, dc, t4 * P:(t4 + 1) * P], rhs=wo[:, dc, hf_ * 512:(hf_ + 1) * 512],
                                                                              start=(dc == 0), stop=(dc == KT - 1)),
                           reads=[r_mT, r_wo], writes=[r_ps[pb]], inc=(dc == KT - 1))
                resid_tile(rc, [4, 5], mod[:, 2, :], src, tt, r_out_tiles, nctx, mod[:, 4, :], mod[:, 3, :], hT[1], r_hT[1], 6 + (tt % 2))
        S.barrier()
        ar.release(m)

    def phase_C(l):
        m = ar.mark()
        build_rope()
        ropeC, ropeS = rope_holder["C"], rope_holder["S"]
        SC = 192 ** -0.5
        qnw = colp[:, CP_QNW:CP_QNW + 3]
        kvnw = colp[:, CP_KVNW:CP_KVNW + 2]
        wkv = ar.alloc([KT, 256], BF16)
        wkr = ar.alloc([KT, 2, P], BF16)
        wcq = ar.alloc([KT, 384], BF16)
        wukv_k = ar.alloc([2, 4, P], BF16)
        wukv_v = ar.alloc([2, 4, P], BF16)
        wuq_n = ar.alloc([3, 4, P], BF16)
        wuq_p = ar.alloc([3, 2, 2, P], BF16)
        r_w = Res()
        ws = win_src(l)
        wload(wkv, ws[:, :, C_CKV:C_CKV + 256], r_w)
        wload(wcq, ws[:, :, C_CQ:C_CQ + 384], r_w)
        for hlf in range(2):
            wload(wkr[:, :, 0, hlf * 64:(hlf + 1) * 64], ws[:, :, C_KR:C_KR + 64], r_w)
            wload(wkr[:, :, 1, hlf * 64:(hlf + 1) * 64], w_krsw[l].rearrange("(kt p) n -> p kt n", p=P), r_w)
        ukv = w_ukv[l].rearrange("(c p) (h two e) -> p c h two e", p=P, h=4, two=2)
        for c in range(2):
            wload(wukv_k[:, c], ukv[:, c, :, 0, :], r_w)
            wload(wukv_v[:, c], ukv[:, c, :, 1, :], r_w)
        uq = w_uq[l].rearrange("(c p) (h e) -> p c h e", p=P, h=4)
        for c in range(3):
            wload(wuq_n[:, c], uq[:, c, :, 0:128], r_w)
        uqs = w_uqsw[l].rearrange("(c p) (h e) -> p c h e", p=P, h=4)
        for pr in range(2):
            for hh in range(2):
                wload(wuq_p[:, :, pr, 0, hh * 64:(hh + 1) * 64], uq[:, :, pr * 2 + hh, 128:192], r_w)
                wload(wuq_p[:, :, pr, 1, hh * 64:(hh + 1) * 64], uqs[:, :, pr * 2 + hh, :], r_w)
        knT = ar.alloc([4, SEQ], BF16)
        kpe = ar.alloc([SEQ], BF16)
        vx = ar.alloc([TT, 4, 132], BF16)
        r_kv = Res()
        op("dve", lambda e: e.memset(vx, 1.0), writes=[r_kv])
        cw = ar.alloc([3, 512], BF16)
        sq = ar.alloc([3, 512], BF16)
        r_cw = Res()
        rbc = ar.alloc([512], F32)
        r_rbc = Res()
        rcol = ar.alloc([8], F32)
        r_rcol = Res()
        t1 = ar.alloc([512], F32)
        t2 = ar.alloc([512], F32)
        r_t1 = Res()
        r_t2 = Res()

        def lat_chunk(wcols, nchunk, nw, tc, width):
            c0 = PAD + tc * 512
            for c in range(nchunk):
                pb = c % 2
                for kt in range(KT):
                    op("pe", lambda e, kt=kt, c=c, pb=pb: e.matmul(PS[pb], lhsT=wcols[:, kt, c * P:(c + 1) * P], rhs=hT[0][:, kt, c0:c0 + 512], start=(kt == 0), stop=(kt == KT - 1)),
                       reads=[r_w, r_hT[0]], writes=[r_ps[pb]], inc=(kt == KT - 1))
                op("act", lambda e, c=c, pb=pb: e.activation(out=cw[:, c, :], in_=PS[pb], func=AF.Copy, scale=nw[:, c:c + 1]), reads=[r_ps[pb], r_colp], writes=[r_cw])
                op("act", lambda e, c=c, pb=pb: e.activation(out=sq[:, c, :], in_=PS[pb], func=AF.Square), reads=[r_ps[pb]], writes=[r_cw])
            for c in range(nchunk):
                op("pe", lambda e, c=c: e.matmul(PS[2], lhsT=onesb, rhs=sq[:, c, :], start=(c == 0), stop=(c == nchunk - 1)),
                   reads=[r_cw, r_const], writes=[r_ps[2]], inc=(c == nchunk - 1))
            op("dve", lambda e: e.tensor_scalar(out=rbc, in0=PS[2], scalar1=1.0 / width, scalar2=RMS_EPS, op0=ALU.mult, op1=ALU.add), reads=[r_ps[2]], writes=[r_rbc])
            op("act", lambda e: e.activation(out=rbc, in_=rbc, func=AF.Sqrt), reads=[r_rbc], writes=[r_rbc])
            op("dve", lambda e: e.reciprocal(out=rbc, in_=rbc), reads=[r_rbc], writes=[r_rbc])

        def rope_evac(dst, pplain, psw, tc, scale_bc):
            tsl = slice(tc * 512, (tc + 1) * 512)
            op("dve", lambda e: e.tensor_tensor(out=t1, in0=PS[pplain], in1=ropeC[:, tsl], op=ALU.mult), reads=[r_ps[pplain], r_rope], writes=[r_t1])
            op("dve", lambda e: e.tensor_tensor(out=t2, in0=PS[psw], in1=ropeS[:, tsl], op=ALU.mult), reads=[r_ps[psw], r_rope], writes=[r_t2])
            if scale_bc is None:
                op("dve", lambda e: e.tensor_tensor(out=dst, in0=t1, in1=t2, op=ALU.add), reads=[r_t1, r_t2], writes=[r_kv])
            else:
                op("dve", lambda e: e.tensor_tensor(out=t1, in0=t1, in1=t2, op=ALU.add), reads=[r_t1, r_t2], writes=[r_t1])
                op("dve", lambda e: e.tensor_tensor(out=dst, in0=t1, in1=scale_bc, op=ALU.mult), reads=[r_t1, r_rbc], writes=[r_q])

        for tc in range(4):
            c0 = PAD + tc * 512
            lat_chunk(wkv, 2, kvnw, tc, 256.0)
            for t4 in range(4):
                for c in range(2):
                    op("pe", lambda e, c=c, t4=t4: e.matmul(PS[3][:, t4:t4 + 1], lhsT=sq[:, c, t4 * P:(t4 + 1) * P], rhs=onesb[:, 0:1], start=(c == 0), stop=(c == 1)),
                       reads=[r_cw, r_const], writes=[r_ps[3]], inc=(c == 1 and t4 == 3))
            op("dve", lambda e: e.tensor_scalar(out=rcol[:, 0:4], in0=PS[3][:, 0:4], scalar1=1.0 / 256, scalar2=RMS_EPS, op0=ALU.mult, op1=ALU.add), reads=[r_ps[3]], writes=[r_rcol])
            op("act", lambda e: e.activation(out=rcol[:, 0:4], in_=rcol[:, 0:4], func=AF.Sqrt), reads=[r_rcol], writes=[r_rcol])
            op("dve", lambda e: e.reciprocal(out=rcol[:, 4:8], in_=rcol[:, 0:4]), reads=[r_rcol], writes=[r_rcol])
            for h in range(4):
                pb = 4 + h % 2
                for c in range(2):
                    op("pe", lambda e, c=c, h=h, pb=pb: e.matmul(PS[pb], lhsT=wukv_k[:, c, h, :], rhs=cw[:, c, :], start=(c == 0), stop=(c == 1)),
                       reads=[r_w, r_cw], writes=[r_ps[pb]], inc=(c == 1))
                op("dve", lambda e, h=h, pb=pb: e.tensor_tensor(out=knT[:, h, tc * 512:(tc + 1) * 512], in0=PS[pb], in1=rbc, op=ALU.mult),
                   reads=[r_ps[pb], r_rbc], writes=[r_kv])
            for t4 in range(4):
                tt = tc * 4 + t4
                pb = 6 + t4 % 2
                for c in range(2):
                    op("pe", lambda e, c=c, t4=t4, pb=pb: e.matmul(PS[pb], lhsT=cw[:, c, t4 * P:(t4 + 1) * P], rhs=wukv_v[:, c, :, :], start=(c == 0), stop=(c == 1)),
                       reads=[r_w, r_cw], writes=[r_ps[pb]], inc=(c == 1))
                op("act", lambda e, tt=tt, t4=t4, pb=pb: e.activation(out=vx[:, tt, :, 0:P], in_=PS[pb].rearrange("p (h e) -> p h e", h=4), func=AF.Copy, scale=rcol[:, 4 + t4:5 + t4]),
                   reads=[r_ps[pb], r_rcol], writes=[r_kv])
            for j in range(2):
                for kt in range(KT):
                    op("pe", lambda e, kt=kt, j=j: e.matmul(PS[j], lhsT=wkr[:, kt, j, :], rhs=hT[0][:, kt, c0:c0 + 512], start=(kt == 0), stop=(kt == KT - 1)),
                       reads=[r_w, r_hT[0]], writes=[r_ps[j]], inc=(kt == KT - 1))
            r_q = r_kv
            rope_evac(kpe[:, tc * 512:(tc + 1) * 512], 0, 1, tc, None)

        qn = ar.alloc([4, 512], BF16)
        qp = ar.alloc([2, 512], BF16)
        r_q = Res()
        pT = [ar.alloc([512], BF16) for _ in range(2)]
        r_pT = [Res(), Res()]
        oc = ar.alloc([4, P], BF16)
        r_oc = Res()
        ocT = ar.alloc([4, 512], BF16)
        r_ocT = Res()
        rs = ar.alloc([4], F32)
        r_rs = Res()
        dstv = obr[2].rearrange("(h p) t -> p h t", p=P)
        ei = 0
        for tc in range(4):
            lat_chunk(wcq, 3, qnw, tc, 384.0)
            for h in range(4):
                pb = h % 2
                for c in range(3):
                    op("pe", lambda e, c=c, h=h, pb=pb: e.matmul(PS[pb], lhsT=wuq_n[:, c, h, :], rhs=cw[:, c, :], start=(c == 0), stop=(c == 2)),
                       reads=[r_w, r_cw], writes=[r_ps[pb]], inc=(c == 2))
                op("dve", lambda e, h=h, pb=pb: e.tensor_tensor(out=qn[:, h, :], in0=PS[pb], in1=rbc, op=ALU.mult), reads=[r_ps[pb], r_rbc], writes=[r_q])
            for pr in range(2):
                for j in range(2):
                    for c in range(3):
                        op("pe", lambda e, c=c, pr=pr, j=j: e.matmul(PS[j], lhsT=wuq_p[:, c, pr, j, :], rhs=cw[:, c, :], start=(c == 0), stop=(c == 2)),
                           reads=[r_w, r_cw], writes=[r_ps[j]], inc=(c == 2))
                rope_evac(qp[:, pr, :], 0, 1, tc, rbc)
            for h in range(4):
                pr, hh = h // 2, h % 2
                rows = slice(hh * 64, (hh + 1) * 64)
                for kt in range(TT):
                    k = ei % 2
                    ei += 1
                    ps_ = 2 + k
                    op("pe", lambda e, kt=kt, h=h, ps_=ps_: e.matmul(PS[ps_], lhsT=knT[:, h, kt * P:(kt + 1) * P], rhs=qn[:, h, :], start=True, stop=False),
                       reads=[r_kv, r_q], writes=[r_ps[ps_]], inc=False)
                    op("pe", lambda e, kt=kt, ps_=ps_: e.matmul(PS[ps_], lhsT=kpe[rows, kt * P:(kt + 1) * P], rhs=qp[rows, pr, :], start=False, stop=True),
                       reads=[r_kv, r_q], writes=[r_ps[ps_]], inc=True)
                    op("act", lambda e, k=k, ps_=ps_: e.activation(out=pT[k], in_=PS[ps_], func=AF.Exp, scale=SC), reads=[r_ps[ps_]], writes=[r_pT[k]])
                    for qt in range(4):
                        op("pe", lambda e, qt=qt, kt=kt, k=k, h=h: e.matmul(PS[4 + qt][:, 0:129], lhsT=pT[k][:, qt * P:(qt + 1) * P], rhs=vx[:, kt, h, 0:129],
                                                                        start=(kt == 0), stop=(kt == TT - 1)),
                           reads=[r_pT[k], r_kv], writes=[r_ps[4 + qt]], inc=(kt == TT - 1 or qt == 3))
                for qt in range(4):
                    op("dve", lambda e, qt=qt: e.reciprocal(out=rs[:, qt:qt + 1], in_=PS[4 + qt][:, 128:129]), reads=[r_ps[4 + qt]], writes=[r_rs])
                    op("dve", lambda e, qt=qt: e.tensor_scalar(out=oc[:, qt, :], in0=PS[4 + qt][:, 0:128], scalar1=rs[:, qt:qt + 1], scalar2=None, op0=ALU.mult),
                       reads=[r_ps[4 + qt], r_rs], writes=[r_oc])
                for qt in range(4):
                    op("pe", lambda e, qt=qt: e.transpose(PSB[0][:, qt * P:(qt + 1) * P], oc[:, qt, :], identb), reads=[r_oc, r_const], writes=[r_ps[0]], inc=(qt == 3))
                op("act", lambda e, h=h: e.activation(out=ocT[:, h, :], in_=PSB[0][:, 0:512], func=AF.Copy), reads=[r_ps[0]], writes=[r_ocT])
            S.dma("sp", dstv[:, :, tc * 512:(tc + 1) * 512], ocT, reads=[r_ocT], writes=[r_obr[2]])
        S.barrier()
        ar.release(m)

    ofw = nc.dram_tensor("ofw", [SEQ, 512], F32, kind="Internal").ap()
    r_ofw = [Res() for _ in range(TT)]
    AX = mybir.AxisListType

    def phase_B(l):
        m = ar.mark()
        ws = win_src(l)
        qkvT = ar.alloc([12, SEQ], BF16)
        r_qkv = Res()
        gb = ar.alloc([TT, 16], F32)
        r_gb = Res()
        dcw = colp[:, CP_DNCONV:CP_DNCONV + 60].rearrange("p (a b) -> p a b", a=12)
        m1 = ar.mark()
        wq = [ar.alloc([KT, P], BF16) for _ in range(2)]
        r_wq = [Res(), Res()]
        acc = ar.alloc([512], F32)
        r_acc = Res()
        sil = ar.alloc([512], F32)
        r_sil = Res()
        sqb = ar.alloc([512], BF16)
        r_sqb = Res()
        rst = ar.alloc([512], F32)
        r_rst = Res()
        wi = 0
        for cc in range(12):
            b = wi % 2
            wi += 1
            wload(wq[b], ws[:, :, C_QKV + cc * P:C_QKV + (cc + 1) * P], r_wq[b])
            for tc in range(4):
                col0 = PAD + tc * 512 - 2
                k = (cc * 4 + tc) % 2
                pw = psum[:, k * 1024:(k + 1) * 1024]
                r_pw_ = [r_ps[2 * k], r_ps[2 * k + 1]]
                for wi_, (o, n) in enumerate([(0, 512), (512, 4)]):
                    for kt in range(KT):
                        op("pe", lambda e, kt=kt, o=o, n=n, pw=pw: e.matmul(pw[:, o:o + n], lhsT=wq[b][:, kt, :], rhs=hT[0][:, kt, col0 + o:col0 + o + n], start=(kt == 0), stop=(kt == KT - 1)),
                           reads=[r_wq[b], r_hT[0]], writes=r_pw_, inc=(kt == KT - 1 and wi_ == 1))
                op("act", lambda e, pw=pw: e.activation(out=acc, in_=pw[:, 2:514], func=AF.Copy, scale=dcw[:, cc, 2:3]), reads=r_pw_ + [r_colp], writes=[r_acc])
                for kk_ in (0, 1, 3, 4):
                    op("dve", lambda e, kk_=kk_, pw=pw: e.scalar_tensor_tensor(out=acc, in0=pw[:, kk_:kk_ + 512], scalar=dcw[:, cc, kk_:kk_ + 1], in1=acc, op0=ALU.mult, op1=ALU.add),
                       reads=r_pw_ + [r_colp, r_acc], writes=[r_acc])
                dst = qkvT[:, cc, tc * 512:(tc + 1) * 512]
                if cc >= 8:
                    op("act", lambda e, dst=dst: e.activation(out=dst, in_=acc, func=AF.Silu), reads=[r_acc], writes=[r_qkv])
                else:
                    op("act", lambda e: e.activation(out=sil, in_=acc, func=AF.Silu), reads=[r_acc], writes=[r_sil])
                    op("act", lambda e: e.activation(out=sqb, in_=sil, func=AF.Square), reads=[r_sil], writes=[r_sqb])
                    op("pe", lambda e: e.matmul(PS[4], lhsT=onesb, rhs=sqb, start=True, stop=True), reads=[r_sqb, r_const], writes=[r_ps[4]])
                    mul = 128.0 if cc < 4 else 1.0
                    op("dve", lambda e, mul=mul: e.tensor_scalar(out=rst, in0=PS[4], scalar1=mul, scalar2=mul * 1e-6, op0=ALU.mult, op1=ALU.add), reads=[r_ps[4]], writes=[r_rst])
                    op("act", lambda e: e.activation(out=rst, in_=rst, func=AF.Sqrt), reads=[r_rst], writes=[r_rst])
                    op("dve", lambda e: e.reciprocal(out=rst, in_=rst), reads=[r_rst], writes=[r_rst])
                    op("dve", lambda e, dst=dst: e.tensor_tensor(out=dst, in0=sil, in1=rst, op=ALU.mult), reads=[r_sil, r_rst], writes=[r_qkv])
        S.barrier()
        ar.release(m1)
        wz = ar.alloc([KT, 528], BF16)
        r_wz = Res()
        wload(wz[:, :, 0:512], ws[:, :, C_Z:C_Z + 512], r_wz)
        wload(wz[:, :, 512:528], ws[:, :, C_BETA:C_BETA + 16], r_wz)
        abias = ar.alloc([16], F32)
        r_ab = Res()
        S.dma("sp", abias, rowp_d[l, :, RP_ALOG:RP_ALOG + 16].partition_broadcast(P), writes=[r_ab])
        op("act", lambda e: e.activation(out=abias[:, 0:8], in_=abias[:, 0:8], func=AF.Exp), reads=[r_ab], writes=[r_ab])
        op("dve", lambda e: e.tensor_scalar(out=abias[:, 0:8], in0=abias[:, 0:8], scalar1=-1.0, scalar2=None, op0=ALU.mult), reads=[r_ab], writes=[r_ab])
        onw = ar.alloc([P], F32)
        r_onw = Res()
        S.dma("sp", onw, rowp_d[l, :, RP_ONW:RP_ONW + P].partition_broadcast(P), writes=[r_onw])
        tmp8 = ar.alloc([32], F32)
        r_t8 = Res()
        for tt in range(TT):
            c0 = PAD + tt * P
            for kt in range(KT):
                op("pe", lambda e, kt=kt: e.matmul(PS[5][:, 0:16], lhsT=hT[0][:, kt, c0:c0 + P], rhs=wz[:, kt, 512:528], start=(kt == 0), stop=(kt == KT - 1)),
                   reads=[r_wz, r_hT[0]], writes=[r_ps[5]], inc=(kt == KT - 1))
            op("act", lambda e, tt=tt: e.activation(out=gb[:, tt, 0:8], in_=PS[5][:, 0:8], func=AF.Sigmoid), reads=[r_ps[5]], writes=[r_gb])
            op("dve", lambda e: e.tensor_tensor(out=tmp8[:, 0:8], in0=PS[5][:, 8:16], in1=abias[:, 8:16], op=ALU.add), reads=[r_ps[5], r_ab], writes=[r_t8])
            op("act", lambda e: e.activation(out=tmp8[:, 8:16], in_=tmp8[:, 0:8], func=AF.Abs), reads=[r_t8], writes=[r_t8])
            op("act", lambda e: e.activation(out=tmp8[:, 8:16], in_=tmp8[:, 8:16], func=AF.Exp, scale=-1.0), reads=[r_t8], writes=[r_t8])
            op("act", lambda e: e.activation(out=tmp8[:, 8:16], in_=tmp8[:, 8:16], func=AF.Ln, bias=1.0), reads=[r_t8], writes=[r_t8])
            op("dve", lambda e: e.scalar_tensor_tensor(out=tmp8[:, 16:24], in0=tmp8[:, 0:8], scalar=0.0, in1=tmp8[:, 8:16], op0=ALU.max, op1=ALU.add), reads=[r_t8], writes=[r_t8])
            op("dve", lambda e, tt=tt: e.tensor_tensor(out=gb[:, tt, 8:16], in0=tmp8[:, 16:24], in1=abias[:, 0:8], op=ALU.mult), reads=[r_t8, r_ab], writes=[r_gb])
        onesf = ar.alloc([P], F32)
        negf = ar.alloc([P], F32)
        incl = [ar.alloc([P], F32) for _ in range(2)]
        negm = [ar.alloc([P], F32) for _ in range(2)]
        strict = [ar.alloc([P], F32) for _ in range(2)]
        r_mk = Res()
        op("pool", lambda e: e.memset(onesf, 1.0), writes=[r_mk])
        op("pool", lambda e: e.memset(negf, -1.0), writes=[r_mk])
        op("pool", lambda e: e.affine_select(out=incl[0], in_=onesf, pattern=[[-1, P]], compare_op=ALU.is_ge, fill=0.0, base=0, channel_multiplier=1), reads=[r_mk], writes=[r_mk])
        op("pool", lambda e: e.affine_select(out=incl[1], in_=onesf, pattern=[[1, P]], compare_op=ALU.is_ge, fill=0.0, base=0, channel_multiplier=-1), reads=[r_mk], writes=[r_mk])
        for d_ in range(2):
            op("dve", lambda e, d_=d_: e.tensor_scalar(out=negm[d_], in0=incl[d_], scalar1=-1.0, scalar2=1e30, op0=ALU.add, op1=ALU.mult), reads=[r_mk], writes=[r_mk])
            op("dve", lambda e, d_=d_: e.tensor_tensor(out=strict[d_], in0=incl[d_], in1=identf, op=ALU.subtract), reads=[r_mk, r_const], writes=[r_mk])
        H4 = [4, P]
        sc = ar.alloc([24], F32); r_sc = Res()
        gtri = ar.alloc(H4, F32); r_gtri = Res()
        Dm = ar.alloc(H4, F32); r_Dm = Res()
        Dms = ar.alloc(H4, F32); r_Dms = Res()
        At = ar.alloc(H4, BF16); r_At = Res()
        Ufb = ar.alloc(H4, BF16); r_Ufb = Res()
        ar2 = Arena.__new__(Arena)
        ar2.t, ar2.off, ar2.nbytes = ar.t, hT_off[1], hT_off[1] + KT * HTW * 2
        L32 = ar2.alloc(H4, F32); r_L32 = Res()
        M32 = ar2.alloc(H4, F32); r_M32 = Res()
        LP = [ar2.alloc(H4, F32) for _ in range(2)]; r_LP = [Res(), Res()]
        MP = [ar2.alloc(H4, F32) for _ in range(2)]; r_MP = [Res(), Res()]
        Tm = [ar2.alloc(H4, F32) for _ in range(2)]; r_Tm = [Res(), Res()]
        Um = [ar2.alloc(H4, F32) for _ in range(2)]; r_Um = [Res(), Res()]
        Lo1 = ar2.alloc(H4, F32); Mo1 = ar2.alloc(H4, F32); Lo2 = ar2.alloc(H4, F32); r_Lo = Res()
        Xa = ar2.alloc(H4, F32); r_Xa = Res()
        Xb = ar2.alloc(H4, F32); r_Xb = Res()
        BD, OF1, OF2 = cst[:, 4:132], cst[:, 132:260], cst[:, 260:388]
        AtT = ar.alloc(H4, BF16); r_AtT = Res()
        kbg = ar.alloc(H4, BF16); kd = ar.alloc(H4, BF16); vb = ar.alloc(H4, BF16); r_kv3 = Res()
        u4 = ar.alloc(H4, F32); r_u4 = Res()
        wT4 = ar.alloc(H4, BF16); r_wT = Res()
        vn4 = ar.alloc(H4, BF16); r_vn = Res()
        t4 = ar.alloc(H4, F32); r_t4 = Res()
        och = ar.alloc(H4, F32); r_och = Res()
        S4 = ar.alloc(H4, F32); Sb4 = ar.alloc(H4, BF16); r_S = Res(); r_Sb = Res()
        ofl = ar.alloc([512], F32); r_ofl = Res()
        ss4 = ar.alloc([12], F32); r_ss4 = Res()
        szb = ar.alloc([512], F32); r_sz = Res()
        obb = ar.alloc([512], BF16); r_obb = Res()
        obT = ar.alloc(H4, BF16); r_obT = Res()
        bank = [0]

        def nb():
            bank[0] = (bank[0] + 1) % 8
            return bank[0]
        bc_h = lambda a: a.unsqueeze(1).to_broadcast([P, 4, P])
        bc_e = lambda a: a.unsqueeze(2).to_broadcast([P, 4, P])
        v4 = lambda pb: PS[pb].rearrange("p (h e) -> p h e", h=4)
        v4b = lambda pb: PSB[pb][:, 0:512].rearrange("p (h e) -> p h e", h=4)
        dstv = obr[1].rearrange("(h p) t -> p h t", p=P)
        for d_ in range(2):
            op("dve", lambda e: e.memset(S4, 0.0), writes=[r_S])
            op("dve", lambda e: e.memset(Sb4, 0.0), writes=[r_Sb])
            tric = incl[1 - d_]
            for step in range(TT):
                c = step if d_ == 0 else TT - 1 - step
                tsl = slice(c * P, (c + 1) * P)
                qTh = lambda h: qkvT[:, h, tsl]
                kTh = lambda h: qkvT[:, 4 + h, tsl]
                vTh = lambda h: qkvT[:, 8 + h, tsl]
                beta4 = gb[:, c, d_ * 4:d_ * 4 + 4]
                g4 = gb[:, c, 8 + d_ * 4:8 + d_ * 4 + 4]
                p1 = nb()
                op("pe", lambda e: e.matmul(PS[p1][:, 0:4], lhsT=tric, rhs=g4, start=True, stop=True), reads=[r_mk, r_gb], writes=[r_ps[p1]], inc=False)
                op("pe", lambda e: e.matmul(PS[p1][:, 4:8], lhsT=onesf, rhs=g4, start=True, stop=True), reads=[r_mk, r_gb], writes=[r_ps[p1]])
                op("dve", lambda e: e.tensor_copy(out=sc[:, 0:8], in_=PS[p1][:, 0:8]), reads=[r_ps[p1]], writes=[r_sc])
                op("act", lambda e: e.activation(out=sc[:, 8:12], in_=sc[:, 0:4], func=AF.Exp), reads=[r_sc], writes=[r_sc])
                op("dve", lambda e: e.tensor_tensor(out=sc[:, 12:16], in0=sc[:, 4:8], in1=sc[:, 0:4], op=ALU.subtract), reads=[r_sc], writes=[r_sc])
                op("act", lambda e: e.activation(out=sc[:, 12:16], in_=sc[:, 12:16], func=AF.Exp), reads=[r_sc], writes=[r_sc])
                op("act", lambda e: e.activation(out=sc[:, 16:20], in_=sc[:, 4:8], func=AF.Exp), reads=[r_sc], writes=[r_sc])
                op("dve", lambda e: e.tensor_tensor(out=sc[:, 20:24], in0=sc[:, 8:12], in1=beta4, op=ALU.mult), reads=[r_sc, r_gb], writes=[r_sc])
                eg, ekd, etot, bge = sc[:, 8:12], sc[:, 12:16], sc[:, 16:20], sc[:, 20:24]
                D0 = (d_ == 0 and step == 0)
                if D0:
                    dbg("gb0", gb[:, 0, :], r_gb, 16)
                    dbg("sc", sc, r_sc, 24)
                    dbg("qT", qkvT[:, 0, 0:512], r_qkv, 512)
                    dbg("kT", qkvT[:, 4, 0:512], r_qkv, 512)
                    dbg("vT", qkvT[:, 8, 0:512], r_qkv, 512)
                for h in range(4):
                    op("dve", lambda e, h=h: e.tensor_scalar(out=gtri[:, h, :], in0=tric, scalar1=g4[:, h:h + 1], scalar2=None, op0=ALU.mult), reads=[r_mk, r_gb], writes=[r_gtri])
                p2 = nb()
                for h in range(4):
                    op("pe", lambda e, h=h: e.matmul(PS[p2][:, h * P:(h + 1) * P], lhsT=gtri[:, h, :], rhs=onesf, start=True, stop=False), reads=[r_gtri, r_mk], writes=[r_ps[p2]], inc=False)
                    op("pe", lambda e, h=h: e.matmul(PS[p2][:, h * P:(h + 1) * P], lhsT=negf, rhs=gtri[:, h, :], start=False, stop=True), reads=[r_gtri, r_mk], writes=[r_ps[p2]], inc=(h == 3))
                op("dve", lambda e: e.tensor_tensor(out=Dm, in0=v4(p2), in1=bc_h(negm[d_]), op=ALU.add), reads=[r_ps[p2], r_mk], writes=[r_Dm])
                op("act", lambda e: e.activation(out=Dm, in_=Dm, func=AF.Exp), reads=[r_Dm], writes=[r_Dm])
                op("dve", lambda e: e.tensor_tensor(out=Dms, in0=Dm, in1=bc_h(strict[d_]), op=ALU.mult), reads=[r_Dm, r_mk], writes=[r_Dms])
                if D0:
                    dbg("Dm", Dm.rearrange("p h e -> p (h e)"), r_Dm, 512)
                p3, p4 = nb(), nb()
                for h in range(4):
                    op("pe", lambda e, h=h: e.matmul(PS[p3][:, h * P:(h + 1) * P], lhsT=kTh(h), rhs=kTh(h), start=True, stop=True), reads=[r_qkv], writes=[r_ps[p3]], inc=(h == 3))
                for h in range(4):
                    op("pe", lambda e, h=h: e.matmul(PS[p4][:, h * P:(h + 1) * P], lhsT=qTh(h), rhs=kTh(h), start=True, stop=True), reads=[r_qkv], writes=[r_ps[p4]], inc=(h == 3))
                for h in range(4):
                    op("dve", lambda e, h=h: e.scalar_tensor_tensor(out=L32[:, h, :], in0=PS[p3][:, h * P:(h + 1) * P], scalar=beta4[:, h:h + 1], in1=Dms[:, h, :], op0=ALU.mult, op1=ALU.mult),
                       reads=[r_ps[p3], r_gb, r_Dms], writes=[r_L32])
                op("dve", lambda e: e.tensor_tensor(out=At, in0=v4(p4), in1=Dm, op=ALU.mult), reads=[r_ps[p4], r_Dm], writes=[r_At])
                p5, p6 = nb(), nb()
                for h in range(4):
                    op("pe", lambda e, h=h: e.transpose(PS[p5][:, h * P:(h + 1) * P], L32[:, h, :], identf), reads=[r_L32, r_const], writes=[r_ps[p5]], inc=(h == 3))
                for h in range(4):
                    op("pe", lambda e, h=h: e.transpose(PSB[p6][:, h * P:(h + 1) * P], At[:, h, :], identb), reads=[r_At, r_const], writes=[r_ps[p6]], inc=(h == 3))
                op("act", lambda e: e.activation(out=M32, in_=v4(p5), func=AF.Copy), reads=[r_ps[p5]], writes=[r_M32])
                op("act", lambda e: e.activation(out=AtT, in_=v4b(p6), func=AF.Copy), reads=[r_ps[p6]], writes=[r_AtT])
                op("dve", lambda e: e.tensor_tensor(out=LP[0], in0=L32, in1=bc_h(BD), op=ALU.mult), reads=[r_L32, r_const], writes=[r_LP[0]])
                op("dve", lambda e: e.tensor_tensor(out=MP[0], in0=M32, in1=bc_h(BD), op=ALU.mult), reads=[r_M32, r_const], writes=[r_MP[0]])
                op("dve", lambda e: e.tensor_tensor(out=Lo1, in0=L32, in1=bc_h(OF1), op=ALU.mult), reads=[r_L32, r_const], writes=[r_Lo])
                op("dve", lambda e: e.tensor_tensor(out=Mo1, in0=M32, in1=bc_h(OF1), op=ALU.mult), reads=[r_M32, r_const], writes=[r_Lo])
                op("dve", lambda e: e.tensor_tensor(out=Lo2, in0=L32, in1=bc_h(OF2), op=ALU.mult), reads=[r_L32, r_const], writes=[r_Lo])
                op("dve", lambda e: e.tensor_tensor(out=Tm[0], in0=bc_h(identf), in1=LP[0], op=ALU.subtract), reads=[r_const, r_LP[0]], writes=[r_Tm[0]])
                op("dve", lambda e: e.tensor_tensor(out=Um[0], in0=bc_h(identf), in1=MP[0], op=ALU.subtract), reads=[r_const, r_MP[0]], writes=[r_Um[0]])

                def mm4(pb, lhs, r_lhs, rhs, r_rhs):
                    for h in range(4):
                        op("pe", lambda e, h=h: e.matmul(PS[pb][:, h * P:(h + 1) * P], lhsT=lhs[:, h, :], rhs=rhs[:, h, :], start=True, stop=True),
                           reads=[r_lhs, r_rhs], writes=[r_ps[pb]], inc=(h == 3))
                cur = 0
                tcur = 0
                for lev in range(1, 5):
                    nxt = 1 - cur
                    pL, pM = nb(), nb()
                    mm4(pL, MP[cur], r_MP[cur], LP[cur], r_LP[cur])
                    mm4(pM, LP[cur], r_LP[cur], MP[cur], r_MP[cur])
                    op("act", lambda e: e.activation(out=LP[nxt], in_=v4(pL), func=AF.Copy), reads=[r_ps[pL]], writes=[r_LP[nxt]])
                    op("dve", lambda e: e.tensor_copy(out=MP[nxt], in_=v4(pM)), reads=[r_ps[pM]], writes=[r_MP[nxt]])
                    pT, pU = nb(), nb()
                    mm4(pT, MP[nxt], r_MP[nxt], Tm[tcur], r_Tm[tcur])
                    mm4(pU, LP[nxt], r_LP[nxt], Um[tcur], r_Um[tcur])
                    op("dve", lambda e: e.tensor_tensor(out=Tm[1 - tcur], in0=v4(pT), in1=Tm[tcur], op=ALU.add), reads=[r_ps[pT], r_Tm[tcur]], writes=[r_Tm[1 - tcur]])
                    op("dve", lambda e: e.tensor_tensor(out=Um[1 - tcur], in0=v4(pU), in1=Um[tcur], op=ALU.add), reads=[r_ps[pU], r_Um[tcur]], writes=[r_Um[1 - tcur]])
                    tcur = 1 - tcur
                    cur = nxt
                Td, r_Td, Ud, r_Ud = Tm[tcur], r_Tm[tcur], Um[tcur], r_Um[tcur]
                T64, r_T64, U64, r_U64 = Tm[1 - tcur], r_Tm[1 - tcur], Um[1 - tcur], r_Um[1 - tcur]
                pX, pXp = nb(), nb()
                mm4(pX, Mo1, r_Lo, Td, r_Td)
                mm4(pXp, Lo1, r_Lo, Ud, r_Ud)
                op("act", lambda e: e.activation(out=Xa, in_=v4(pX), func=AF.Copy), reads=[r_ps[pX]], writes=[r_Xa])
                op("dve", lambda e: e.tensor_copy(out=Xb, in_=v4(pXp)), reads=[r_ps[pXp]], writes=[r_Xb])
                pY, pYp = nb(), nb()
                mm4(pY, Ud, r_Ud, Xa, r_Xa)
                mm4(pYp, Td, r_Td, Xb, r_Xb)
                op("dve", lambda e: e.tensor_tensor(out=T64, in0=Td, in1=v4(pY), op=ALU.subtract), reads=[r_ps[pY], r_Td], writes=[r_T64])
                op("dve", lambda e: e.tensor_tensor(out=U64, in0=Ud, in1=v4(pYp), op=ALU.subtract), reads=[r_ps[pYp], r_Ud], writes=[r_U64])
                pXp = nb()
                mm4(pXp, Lo2, r_Lo, U64, r_U64)
                op("act", lambda e: e.activation(out=Xb, in_=v4(pXp), func=AF.Copy), reads=[r_ps[pXp]], writes=[r_Xb])
                pYp = nb()
                mm4(pYp, T64, r_T64, Xb, r_Xb)
                op("dve", lambda e: e.tensor_tensor(out=Ufb, in0=U64, in1=v4(pYp), op=ALU.subtract), reads=[r_ps[pYp], r_U64], writes=[r_Ufb])
                Uf, r_Uf = Ufb, r_Ufb
                if D0:
                    dbg("U", Uf.rearrange("p h e -> p (h e)"), r_Uf, 512)
                p7, p8 = nb(), nb()
                for h in range(4):
                    op("pe", lambda e, h=h: e.transpose(PSB[p7][:, h * P:(h + 1) * P], kTh(h), identb), reads=[r_qkv, r_const], writes=[r_ps[p7]], inc=(h == 3))
                for h in range(4):
                    op("pe", lambda e, h=h: e.transpose(PSB[p8][:, h * P:(h + 1) * P], vTh(h), identb), reads=[r_qkv, r_const], writes=[r_ps[p8]], inc=(h == 3))
                op("dve", lambda e: e.tensor_tensor(out=kbg, in0=v4b(p7), in1=bc_e(bge), op=ALU.mult), reads=[r_ps[p7], r_sc], writes=[r_kv3])
                op("dve", lambda e: e.tensor_tensor(out=kd, in0=v4b(p7), in1=bc_e(ekd), op=ALU.mult), reads=[r_ps[p7], r_sc], writes=[r_kv3])
                op("dve", lambda e: e.tensor_tensor(out=vb, in0=v4b(p8), in1=bc_e(beta4), op=ALU.mult), reads=[r_ps[p8], r_gb], writes=[r_kv3])
                p9, p10 = nb(), nb()
                for h in range(4):
                    op("pe", lambda e, h=h: e.matmul(PS[p9][:, h * P:(h + 1) * P], lhsT=Uf[:, h, :], rhs=vb[:, h, :], start=True, stop=True), reads=[r_Uf, r_kv3], writes=[r_ps[p9]], inc=(h == 3))
                for h in range(4):
                    op("pe", lambda e, h=h: e.matmul(PS[p10][:, h * P:(h + 1) * P], lhsT=kbg[:, h, :], rhs=Uf[:, h, :], start=True, stop=True), reads=[r_Uf, r_kv3], writes=[r_ps[p10]], inc=(h == 3))
                op("act", lambda e: e.activation(out=u4, in_=v4(p9), func=AF.Copy), reads=[r_ps[p9]], writes=[r_u4])
                op("act", lambda e: e.activation(out=wT4, in_=v4(p10), func=AF.Copy), reads=[r_ps[p10]], writes=[r_wT])
                p11 = nb()
                for h in range(4):
                    op("pe", lambda e, h=h: e.matmul(PS[p11][:, h * P:(h + 1) * P], lhsT=wT4[:, h, :], rhs=Sb4[:, h, :], start=True, stop=True), reads=[r_wT, r_Sb], writes=[r_ps[p11]], inc=(h == 3))
                op("dve", lambda e: e.tensor_tensor(out=vn4, in0=u4, in1=v4(p11), op=ALU.subtract), reads=[r_u4, r_ps[p11]], writes=[r_vn])
                p12, p13, p14 = nb(), nb(), nb()
                for h in range(4):
                    op("pe", lambda e, h=h: e.matmul(PS[p12][:, h * P:(h + 1) * P], lhsT=qTh(h), rhs=Sb4[:, h, :], start=True, stop=True), reads=[r_qkv, r_Sb], writes=[r_ps[p12]], inc=(h == 3))
                for h in range(4):
                    op("pe", lambda e, h=h: e.matmul(PS[p13][:, h * P:(h + 1) * P], lhsT=AtT[:, h, :], rhs=vn4[:, h, :], start=True, stop=True), reads=[r_AtT, r_vn], writes=[r_ps[p13]], inc=(h == 3))
                for h in range(4):
                    op("pe", lambda e, h=h: e.matmul(PS[p14][:, h * P:(h + 1) * P], lhsT=kd[:, h, :], rhs=vn4[:, h, :], start=True, stop=True), reads=[r_kv3, r_vn], writes=[r_ps[p14]], inc=(h == 3))
                op("dve", lambda e: e.tensor_tensor(out=t4, in0=v4(p12), in1=bc_e(eg), op=ALU.mult), reads=[r_ps[p12], r_sc], writes=[r_t4])
                op("dve", lambda e: e.tensor_tensor(out=och, in0=v4(p13), in1=t4, op=ALU.add), reads=[r_ps[p13], r_t4], writes=[r_och])
                op("dve", lambda e: e.tensor_tensor(out=S4, in0=S4, in1=bc_e(etot), op=ALU.mult), reads=[r_sc, r_S], writes=[r_S])
                op("dve", lambda e: e.tensor_tensor(out=S4, in0=v4(p14), in1=S4, op=ALU.add), reads=[r_ps[p14], r_S], writes=[r_S])
                op("act", lambda e: e.activation(out=Sb4, in_=S4, func=AF.Copy), reads=[r_S], writes=[r_Sb])
                ochf = och.rearrange("p h e -> p (h e)")
                if D0:
                    dbg("u4", u4.rearrange("p h e -> p (h e)"), r_u4, 512)
                    dbg("wT", wT4.rearrange("p h e -> p (h e)"), r_wT, 512)
                    dbg("vn", vn4.rearrange("p h e -> p (h e)"), r_vn, 512)
                    dbg("och", ochf, r_och, 512)
                    dbg("S4", S4.rearrange("p h e -> p (h e)"), r_S, 512)
                if d_ == 0 and step == 1:
                    dbg("och1", ochf, r_och, 512)
                if d_ == 1 and step == 0:
                    dbg("ochb", ochf, r_och, 512)
                if d_ == 0:
                    S.dma("sp", ofw[c * P:(c + 1) * P, :], ochf, reads=[r_och], writes=[r_ofw[c]])
                else:
                    S.dma("sp", ofl, ofw[c * P:(c + 1) * P, :], reads=[r_ofw[c]], writes=[r_ofl])
                    op("dve", lambda e: e.tensor_tensor(out=ochf, in0=ochf, in1=ofl, op=ALU.add), reads=[r_och, r_ofl], writes=[r_och])
                    op("dve", lambda e: e.tensor_tensor(out=t4, in0=och, in1=och, op=ALU.mult), reads=[r_och], writes=[r_t4])
                    op("dve", lambda e: e.reduce_sum(out=ss4[:, 0:4], in_=t4, axis=AX.X), reads=[r_t4], writes=[r_ss4])
                    op("dve", lambda e: e.tensor_scalar(out=ss4[:, 4:8], in0=ss4[:, 0:4], scalar1=1.0 / P, scalar2=RMS_EPS, op0=ALU.mult, op1=ALU.add), reads=[r_ss4], writes=[r_ss4])
                    op("act", lambda e: e.activation(out=ss4[:, 4:8], in_=ss4[:, 4:8], func=AF.Sqrt), reads=[r_ss4], writes=[r_ss4])
                    op("dve", lambda e: e.reciprocal(out=ss4[:, 8:12], in_=ss4[:, 4:8]), reads=[r_ss4], writes=[r_ss4])
                    op("dve", lambda e: e.tensor_tensor(out=t4, in0=och, in1=bc_e(ss4[:, 8:12]), op=ALU.mult), reads=[r_och, r_ss4], writes=[r_t4])
                    op("dve", lambda e: e.tensor_tensor(out=t4, in0=t4, in1=bc_h(onw), op=ALU.mult), reads=[r_t4, r_onw], writes=[r_t4])
                    pz = nb()
                    c0 = PAD + c * P
                    for kt in range(KT):
                        op("pe", lambda e, kt=kt: e.matmul(PS[pz], lhsT=hT[0][:, kt, c0:c0 + P], rhs=wz[:, kt, 0:512], start=(kt == 0), stop=(kt == KT - 1)),
                           reads=[r_wz, r_hT[0]], writes=[r_ps[pz]], inc=(kt == KT - 1))
                    op("act", lambda e: e.activation(out=szb, in_=PS[pz], func=AF.Silu), reads=[r_ps[pz]], writes=[r_sz])
                    op("dve", lambda e: e.tensor_tensor(out=obb, in0=t4.rearrange("p h e -> p (h e)"), in1=szb, op=ALU.mult), reads=[r_t4, r_sz], writes=[r_obb])
                    pt = nb()
                    for h in range(4):
                        op("pe", lambda e, h=h: e.transpose(PSB[pt][:, h * P:(h + 1) * P], obb[:, h * P:(h + 1) * P], identb), reads=[r_obb, r_const], writes=[r_ps[pt]], inc=(h == 3))
                    op("act", lambda e: e.activation(out=obT, in_=v4b(pt), func=AF.Copy), reads=[r_ps[pt]], writes=[r_obT])
                    S.dma("sp", dstv[:, :, c * P:(c + 1) * P], obT, reads=[r_obT], writes=[r_obr[1]])
        S.barrier()
        op("dve", lambda e: e.memset(hT[1][:, :, 0:PAD], 0.0), writes=[r_hT[1]])
        op("dve", lambda e: e.memset(hT[1][:, :, PAD + SEQ:HTW], 0.0), writes=[r_hT[1]])
        S.barrier()
        ar.release(m)

    for l in range(n_layers):
        phase_mod(l)
        src = x_in if l == 0 else out
        phase_norm_from_dram(src, mod[:, 1, :], mod[:, 0, :], hT[0], r_hT[0])
        if "A" in parts:
            phase_A(l)
        else:
            zero_branch(0)
        if "B" in parts:
            phase_B(l)
        else:
            zero_branch(1)
        if "C" in parts:
            phase_C(l)
        else:
            zero_branch(2)
        phase_merge(l, src)
        phase_ffn(l, hT[1], r_hT[1], r_out_tiles, last=True)
    S.barrier()
    S.finish()
    print("nins", S.nins, "nwaits", S.nwaits, "sbuf", ar.off)
    return nc


def _host_inputs(inputs, n_cores=8):
    f = lambda a: np.ascontiguousarray(np.asarray(a))
    x = f(inputs["x"]); c = f(inputs["c"]); pos = f(inputs["positions"]).astype(np.int32)
    w_in = f(inputs["w_in"])
    w_uq = f(inputs["mla_w_uq"])
    Lh = w_uq.shape[0]
    sw = []
    for h in range(4):
        sw.append(w_uq[:, :, h * 192 + 160:h * 192 + 192])
        sw.append(w_uq[:, :, h * 192 + 128:h * 192 + 160])
    w_uqsw = f(np.concatenate(sw, axis=2))
    w_krsw = f(np.concatenate([w_in[:, :, C_KR + 32:C_KR + 64], w_in[:, :, C_KR:C_KR + 32]], axis=2))
    w_spT = f(np.transpose(np.asarray(inputs["a_w_sp"]), (0, 1, 3, 2)))
    colp = np.zeros((Lh, P, NCOL), np.float32)
    colp[:, :, CP_QNW:CP_QNW + 3] = np.asarray(inputs["mla_q_norm_w"]).reshape(Lh, 3, P).transpose(0, 2, 1)
    colp[:, :, CP_KVNW:CP_KVNW + 2] = np.asarray(inputs["mla_kv_norm_w"]).reshape(Lh, 2, P).transpose(0, 2, 1)
    colp[:, :, CP_DNCONV:CP_DNCONV + 60] = np.asarray(inputs["dn_conv_w"]).reshape(Lh, 5, 12, P).transpose(0, 3, 2, 1).reshape(Lh, P, 60)
    colp[:, :, CP_FCW:CP_FCW + 132] = np.asarray(inputs["ffn_conv_w"]).reshape(Lh, 3, 44, P).transpose(0, 3, 2, 1).reshape(Lh, P, 132)
    colp[:, :, CP_FCB:CP_FCB + 44] = np.asarray(inputs["ffn_conv_b"]).reshape(Lh, 44, P).transpose(0, 2, 1)
    colp[:, :, CP_BGATE:CP_BGATE + 24] = np.asarray(inputs["b_gate"]).reshape(Lh, 3, 8, P).transpose(0, 3, 1, 2).reshape(Lh, P, 24)
    colp[:, :, CP_BSP:CP_BSP + 4] = np.asarray(inputs["a_b_sp"]).transpose(0, 2, 1)
    rowp = np.zeros((Lh, 1, NROW), np.float32)
    rowp[:, 0, RP_NW:RP_NW + 4096] = np.asarray(inputs["norm_w"]).reshape(Lh, 4096)
    rowp[:, 0, RP_BADA:RP_BADA + 6144] = np.asarray(inputs["b_ada"])
    rowp[:, 0, RP_LNW:RP_LNW + 512] = np.asarray(inputs["a_ln_w"])
    rowp[:, 0, RP_LNB:RP_LNB + 512] = np.asarray(inputs["a_ln_b"])
    rowp[:, 0, RP_ONW:RP_ONW + 128] = np.asarray(inputs["dn_o_norm_w"])
    rowp[:, 0, RP_ALOG:RP_ALOG + 8] = np.asarray(inputs["dn_a_log"]).reshape(Lh, 8)
    rowp[:, 0, RP_DTB:RP_DTB + 8] = np.asarray(inputs["dn_dt_bias"]).reshape(Lh, 8)
    cst = np.zeros((P, 388), np.float32)
    ii = np.arange(P)[:, None]; jj = np.arange(P)[None, :]
    cst[:, 4:132] = (ii // 32 == jj // 32)
    cst[:, 132:260] = (ii // 64 == jj // 64) & (ii // 32 != jj // 32)
    cst[:, 260:388] = (ii // 64 != jj // 64)
    inv_freq = (10000.0 ** (-np.arange(0, 64, 2, dtype=np.float32) / np.float32(64))).astype(np.float32)
    cst[:, 0] = np.tile(inv_freq, 4)
    cst[:, 1] = np.where((np.arange(P) % 64) < 32, -1.0, 1.0)
    shared = {"w_ada": f(inputs["w_ada"]), "w_in": w_in, "w_uq": w_uq, "w_uqsw": w_uqsw, "w_krsw": w_krsw,
              "w_ukv": f(inputs["mla_w_ukv"]), "w_branch": f(inputs["w_branch"]), "w_o": f(inputs["w_o"]),
              "w_up": f(inputs["ffn_w_up"]), "w_down": f(inputs["ffn_w_down"]), "w_spT": w_spT,
              "colp": colp, "rowp": rowp, "cst": cst}
    maps = []
    for b in range(n_cores):
        mcore = dict(shared)
        mcore["x"] = f(x[b])
        mcore["cT"] = f(c[b].reshape(KT, P).T)
        mcore["pos"] = f(pos[b].reshape(1, SEQ))
        maps.append(mcore)
    return maps


def kernel(**inputs):
    maps = _host_inputs(inputs, 8)
    nc = build()
    res = run_bass_kernel_spmd(nc, maps, core_ids=list(range(8)))
    return np.stack([r["out"] for r in res.results], axis=0).astype(np.float32)
```

```python
import os
import numpy as np
import concourse.bass as bass
import concourse.mybir as mybir
from concourse.bass_utils import run_bass_kernel_spmd

F32 = mybir.dt.float32
BF16 = mybir.dt.bfloat16
U8 = mybir.dt.uint8
I32 = mybir.dt.int32
AF = mybir.ActivationFunctionType
ALU = mybir.AluOpType

P = 128
SEQ = 2048
D = 1024
TT = SEQ // P
KT = D // P
DEPTH = 4
PAD = 2
HTW = SEQ + 2 * PAD
N_IN = 6864
D_FF = 2816
NFC = D_FF // P
C_AUV, C_QKV, C_Z, C_BETA, C_ALPHA, C_CQ, C_CKV, C_KR, C_GATE = 0, 1024, 2560, 3072, 3080, 3088, 3472, 3728, 3792
CP_QNW, CP_KVNW, CP_DNCONV, CP_FCW, CP_FCB, CP_BGATE, CP_BSP = 0, 3, 5, 65, 197, 241, 265
NCOL = 269
RP_NW, RP_BADA, RP_LNW, RP_LNB, RP_ONW, RP_ALOG, RP_DTB = 0, 4096, 10240, 10752, 11264, 11392, 11400
NROW = 11408
RMS_EPS = 1e-6
LN_EPS = 1e-5


class Res:
    __slots__ = ("w", "r")

    def __init__(self):
        self.w = None
        self.r = {}


class Sched:
    def __init__(self, nc, ndma=8):
        self.nc = nc
        self.engs = {}
        for k, e in (("pe", nc.tensor), ("act", nc.scalar), ("dve", nc.vector), ("pool", nc.gpsimd), ("sp", nc.sync)):
            self.engs[k] = dict(eng=e, sem=nc.alloc_semaphore("s_" + k), cnt=0, seen={}, name=k)
        self.slots = {q: [dict(sem=nc.alloc_semaphore(f"d_{q}{i}"), tot=0) for i in range(ndma)] for q in ("sp", "pool")}
        self.rr = {"sp": 0, "pool": 0}
        self.nins = 0
        self.nwaits = 0

    def _wait(self, E, tok):
        sem, val = tok
        if E["seen"].get(sem.num, 0) >= val:
            return
        E["seen"][sem.num] = val
        E["eng"].wait_ge(sem, val)
        self.nwaits += 1

    def _deps(self, E, reads, writes, skip_self):
        toks = {}

        def add(tok):
            if tok is None:
                return
            sem, val = tok
            if skip_self and sem.num == E["sem"].num:
                return
            if sem.num not in toks or toks[sem.num][1] < val:
                toks[sem.num] = tok
        for r in reads:
            add(r.w)
        for w in writes:
            add(w.w)
            for t in w.r.values():
                add(t)
        for tok in toks.values():
            self._wait(E, tok)

    def _commit(self, tok, reads, writes):
        for r in reads:
            r.r[tok[0].num] = tok
        for w in writes:
            w.w = tok
            w.r = {}

    def op(self, ek, fn, reads=(), writes=(), inc=True):
        E = self.engs[ek]
        self._deps(E, reads, writes, skip_self=(ek == "pe"))
        ins = fn(E["eng"])
        self.nins += 1
        if inc:
            E["cnt"] += 1
            ins.then_inc(E["sem"], 1)
            tok = (E["sem"], E["cnt"])
        else:
            tok = (E["sem"], E["cnt"] + 1)
        self._commit(tok, reads, writes)
        return tok

    def dma(self, q, out, in_, reads=(), writes=()):
        E = self.engs[q]
        sl = self.slots[q][self.rr[q]]
        self.rr[q] = (self.rr[q] + 1) % len(self.slots[q])
        if sl["tot"] > 0:
            self._wait(E, (sl["sem"], sl["tot"]))
        self._deps(E, reads, writes, skip_self=False)
        ins = E["eng"].dma_start(out=out, in_=in_)
        sl["tot"] += 16
        ins.then_inc(sl["sem"], 16)
        tok = (sl["sem"], sl["tot"])
        self.nins += 1
        self._commit(tok, reads, writes)
        return tok

    def barrier(self):
        toks = []
        for F in self.engs.values():
            if F["cnt"] > 0:
                toks.append((F["sem"], F["cnt"]))
        for q in self.slots:
            for sl in self.slots[q]:
                if sl["tot"] > 0:
                    toks.append((sl["sem"], sl["tot"]))
        for E in self.engs.values():
            for t in toks:
                if t[0].num != E["sem"].num:
                    self._wait(E, t)

    def finish(self):
        E = self.engs["sp"]
        for q in self.slots:
            for sl in self.slots[q]:
                if sl["tot"] > 0:
                    self._wait(E, (sl["sem"], sl["tot"]))


class Arena:
    def __init__(self, nc, nbytes):
        self.t = nc.alloc_sbuf_tensor("arena", [P, nbytes], U8)
        self.off = 0
        self.nbytes = nbytes

    def alloc(self, shape, dt, parts=P):
        sz = 4 if dt in (F32, I32) else 2
        n = int(np.prod(shape)) * sz
        assert self.off + n <= self.nbytes, ("SBUF arena overflow", self.off, n)
        a = self.t[0:parts, self.off:self.off + n].bitcast(dt)
        self.off += (n + 63) // 64 * 64
        if len(shape) == 2:
            a = a.rearrange("p (a b) -> p a b", a=shape[0])
        elif len(shape) == 3:
            a = a.rearrange("p (a b c) -> p a b c", a=shape[0], b=shape[1])
        elif len(shape) == 4:
            a = a.rearrange("p (a b c d) -> p a b c d", a=shape[0], b=shape[1], c=shape[2])
        return a

    def mark(self):
        return self.off

    def release(self, m):
        self.off = m


def build(n_layers=DEPTH, parts=("A", "B", "C")):
    nc = bass.Bass("TRN2", target_bir_lowering=False)
    L = DEPTH
    dt_in = lambda name, shape, dt=F32: nc.dram_tensor(name, shape, dt, kind="ExternalInput").ap()
    x_in = dt_in("x", [SEQ, D])
    cT_d = dt_in("cT", [P, KT])
    pos_d = dt_in("pos", [1, SEQ], I32)
    w_ada = dt_in("w_ada", [L, D, 6 * D])
    w_in = dt_in("w_in", [L, D, N_IN])
    w_uq = dt_in("w_uq", [L, 384, 768])
    w_uqsw = dt_in("w_uqsw", [L, 384, 256])
    w_krsw = dt_in("w_krsw", [L, D, 64])
    w_ukv = dt_in("w_ukv", [L, 256, 1024])
    w_branch = dt_in("w_branch", [L, 3, 512, D])
    w_o = dt_in("w_o", [L, D, D])
    w_up = dt_in("w_up", [L, D, 2 * D_FF])
    w_down = dt_in("w_down", [L, D_FF, D])
    w_spT = dt_in("w_spT", [L, 4, P, P])
    colp_d = dt_in("colp", [L, P, NCOL])
    rowp_d = dt_in("rowp", [L, 1, NROW])
    cst_d = dt_in("cst", [P, 388])
    out = nc.dram_tensor("out", [SEQ, D], F32, kind="ExternalOutput").ap()

    S = Sched(nc)
    DBG = os.environ.get("KDBG") == "1"
    dbg_t = nc.dram_tensor("dbg", [24, P, 512], F32, kind="ExternalOutput").ap() if DBG else None
    dbg_i = [0]

    def dbg(name, ap, r, width):
        if not DBG or dbg_i[0] >= 24:
            return
        i = dbg_i[0]
        dbg_i[0] += 1
        print("DBGSLOT", i, name, width)
        S.dma("pool", dbg_t[i, :, 0:width], ap, reads=[r], writes=[Res()])
    ar = Arena(nc, 207 * 1024)
    psum = nc.alloc_psum_tensor("ps", [P, 8 * 512], F32)
    PS = [psum[:, b * 512:(b + 1) * 512] for b in range(8)]
    PSB = [PS[b].bitcast(BF16) for b in range(8)]
    r_ps = [Res() for _ in range(8)]

    op = S.op

    identf = ar.alloc([P], F32)
    identb = ar.alloc([P], BF16)
    onesb = ar.alloc([P], BF16)
    cst = ar.alloc([388], F32)
    r_const = Res()
    op("pool", lambda e: e.memset(identf, 0.0), writes=[r_const])
    op("pool", lambda e: e.affine_select(out=identf, in_=identf, pattern=[[-1, P]], compare_op=ALU.not_equal,
                                         fill=1.0, base=0, channel_multiplier=1), reads=[r_const], writes=[r_const])
    op("dve", lambda e: e.tensor_copy(out=identb, in_=identf), reads=[r_const], writes=[r_const])
    op("dve", lambda e: e.memset(onesb, 1.0), writes=[r_const])
    S.dma("sp", cst, cst_d, writes=[r_const])

    hT_off = []
    hT = []
    for _ in range(2):
        hT_off.append(ar.off)
        hT.append(ar.alloc([KT, HTW], BF16))
    r_hT = [Res(), Res()]
    for i in range(2):
        op("pool", lambda e, i=i: e.memset(hT[i], 0.0), writes=[r_hT[i]])
    mod = ar.alloc([6, D], F32)
    r_mod = Res()
    cbf = ar.alloc([KT, P], BF16)
    r_cbf = Res()
    colp = ar.alloc([NCOL], F32)
    r_colp = Res()
    base_mark = ar.mark()

    m0 = ar.mark()
    ctile = ar.alloc([KT], F32)
    r_ct = Res()
    S.dma("sp", ctile, cT_d, writes=[r_ct])
    op("act", lambda e: e.activation(out=ctile, in_=ctile, func=AF.Silu), reads=[r_ct], writes=[r_ct])
    op("dve", lambda e: e.tensor_copy(out=cbf, in_=ctile.unsqueeze(2).to_broadcast([P, KT, P])), reads=[r_ct], writes=[r_cbf])
    S.barrier()
    ar.release(m0)

    def wload(dst, src, r):
        return S.dma("pool", dst, src, writes=[r])

    def phase_mod(l):
        m = ar.mark()
        S.dma("sp", colp, colp_d[l], writes=[r_colp])
        wb = [ar.alloc([KT, 512], BF16) for _ in range(2)]
        r_wb = [Res(), Res()]
        bb = [ar.alloc([512], F32) for _ in range(2)]
        r_bb = [Res(), Res()]
        nwb = ar.alloc([4 * D], F32)
        r_nwb = Res()
        S.dma("sp", nwb, rowp_d[l, :, RP_NW:RP_NW + 4 * D].partition_broadcast(P), writes=[r_nwb])
        modf = mod.rearrange("p a b -> p (a b)")
        wsrc = w_ada[l].rearrange("(kt p) n -> p kt n", p=P)
        for n in range(12):
            b = n % 2
            wload(wb[b], wsrc[:, :, n * 512:(n + 1) * 512], r_wb[b])
            S.dma("sp", bb[b], rowp_d[l, :, RP_BADA + n * 512:RP_BADA + (n + 1) * 512].partition_broadcast(P), writes=[r_bb[b]])
            pb = n % 2
            for kt in range(KT):
                op("pe", lambda e, kt=kt, b=b, pb=pb: e.matmul(PS[pb], lhsT=cbf[:, kt, :], rhs=wb[b][:, kt, :], start=(kt == 0), stop=(kt == KT - 1)),
                   reads=[r_cbf, r_wb[b]], writes=[r_ps[pb]], inc=(kt == KT - 1))
            op("dve", lambda e, n=n, b=b, pb=pb: e.tensor_tensor(out=modf[:, n * 512:(n + 1) * 512], in0=PS[pb], in1=bb[b], op=ALU.add),
               reads=[r_ps[pb], r_bb[b]], writes=[r_mod])
        nw = nwb.rearrange("p (a b) -> p a b", a=4)
        op("dve", lambda e: e.scalar_tensor_tensor(out=mod[:, 1, :], in0=mod[:, 1, :], scalar=1.0, in1=nw[:, 0, :], op0=ALU.add, op1=ALU.mult),
           reads=[r_nwb, r_mod], writes=[r_mod])
        op("dve", lambda e: e.tensor_tensor(out=mod[:, 2, :], in0=mod[:, 2, :], in1=nw[:, 1, :], op=ALU.mult), reads=[r_nwb, r_mod], writes=[r_mod])
        op("dve", lambda e: e.scalar_tensor_tensor(out=mod[:, 4, :], in0=mod[:, 4, :], scalar=1.0, in1=nw[:, 2, :], op0=ALU.add, op1=ALU.mult),
           reads=[r_nwb, r_mod], writes=[r_mod])
        op("dve", lambda e: e.tensor_tensor(out=mod[:, 5, :], in0=mod[:, 5, :], in1=nw[:, 3, :], op=ALU.mult), reads=[r_nwb, r_mod], writes=[r_mod])
        S.barrier()
        ar.release(m)

    def make_norm_ctx():
        ctx = dict(
            ss=ar.alloc([4], F32), r_ss=Res(),
            junk=ar.alloc([D], BF16), r_junk=Res(),
            hf=ar.alloc([D], F32), r_hf=Res(),
            hb=[ar.alloc([D], BF16) for _ in range(2)], r_hb=[Res(), Res()], i=0)
        return ctx

    def norm_tile(ctx, xt, r_xt, A, sh, hdst, r_hdst, tt, psb):
        ss, r_ss = ctx["ss"], ctx["r_ss"]
        k = ctx["i"] % 2
        ctx["i"] += 1
        hb, r_hb = ctx["hb"][k], ctx["r_hb"][k]
        op("act", lambda e: e.activation(out=ctx["junk"], in_=xt, func=AF.Square, accum_out=ss[:, 0:1]), reads=[r_xt], writes=[ctx["r_junk"], r_ss])
        op("dve", lambda e: e.tensor_scalar(out=ss[:, 1:2], in0=ss[:, 0:1], scalar1=1.0 / D, scalar2=RMS_EPS, op0=ALU.mult, op1=ALU.add), reads=[r_ss], writes=[r_ss])
        op("act", lambda e: e.activation(out=ss[:, 2:3], in_=ss[:, 1:2], func=AF.Sqrt), reads=[r_ss], writes=[r_ss])
        op("dve", lambda e: e.reciprocal(out=ss[:, 3:4], in_=ss[:, 2:3]), reads=[r_ss], writes=[r_ss])
        op("dve", lambda e: e.scalar_tensor_tensor(out=ctx["hf"], in0=xt, scalar=ss[:, 3:4], in1=A, op0=ALU.mult, op1=ALU.mult),
           reads=[r_xt, r_ss, r_mod], writes=[ctx["r_hf"]])
        op("dve", lambda e: e.tensor_tensor(out=hb, in0=ctx["hf"], in1=sh, op=ALU.add), reads=[ctx["r_hf"], r_mod], writes=[r_hb])
        for kt in range(KT):
            op("pe", lambda e, kt=kt: e.transpose(PSB[psb][:, kt * P:(kt + 1) * P], hb[:, kt * P:(kt + 1) * P], identb),
               reads=[r_hb, r_const], writes=[r_ps[psb]], inc=(kt == KT - 1))
        op("act", lambda e: e.activation(out=hdst[:, :, PAD + tt * P:PAD + (tt + 1) * P], in_=PSB[psb].rearrange("p (a b) -> p a b", a=KT), func=AF.Copy),
           reads=[r_ps[psb]], writes=[r_hdst])

    def phase_norm_from_dram(src, A, sh, hdst, r_hdst):
        m = ar.mark()
        ctx = make_norm_ctx()
        xt = [ar.alloc([D], F32) for _ in range(2)]
        r_xt = [Res(), Res()]
        for tt in range(TT):
            b = tt % 2
            S.dma("sp", xt[b], src[tt * P:(tt + 1) * P, :], writes=[r_xt[b]])
            norm_tile(ctx, xt[b], r_xt[b], A, sh, hdst, r_hdst, tt, 6 + b)
        S.barrier()
        ar.release(m)

    def make_resid_ctx():
        return dict(xt=[ar.alloc([D], F32) for _ in range(2)], r_xt=[Res(), Res()],
                    ss=ar.alloc([8], F32), r_ss=Res(), junk=ar.alloc([D], BF16), r_junk=Res(),
                    tmp=ar.alloc([D], F32), r_tmp=Res(), i=0)

    def resid_tile(rc, pbanks, G, src, tt, r_out_tiles, nctx=None, nA=None, nsh=None, hdst=None, r_hdst=None, npsb=None):
        k = rc["i"] % 2
        rc["i"] += 1
        xt, r_xt = rc["xt"][k], rc["r_xt"][k]
        ss, r_ss = rc["ss"], rc["r_ss"]
        S.dma("sp", xt, src[tt * P:(tt + 1) * P, :], reads=[r_out_tiles[tt]] if src is out else [], writes=[r_xt])
        for j, pb in enumerate(pbanks):
            op("act", lambda e, j=j, pb=pb: e.activation(out=rc["junk"][:, j * 512:(j + 1) * 512], in_=PS[pb], func=AF.Square, accum_out=ss[:, j:j + 1]),
               reads=[r_ps[pb]], writes=[rc["r_junk"], r_ss])
        op("dve", lambda e: e.tensor_tensor(out=ss[:, 2:3], in0=ss[:, 0:1], in1=ss[:, 1:2], op=ALU.add), reads=[r_ss], writes=[r_ss])
        op("dve", lambda e: e.tensor_scalar(out=ss[:, 3:4], in0=ss[:, 2:3], scalar1=1.0 / D, scalar2=RMS_EPS, op0=ALU.mult, op1=ALU.add), reads=[r_ss], writes=[r_ss])
        op("act", lambda e: e.activation(out=ss[:, 4:5], in_=ss[:, 3:4], func=AF.Sqrt), reads=[r_ss], writes=[r_ss])
        op("dve", lambda e: e.reciprocal(out=ss[:, 5:6], in_=ss[:, 4:5]), reads=[r_ss], writes=[r_ss])
        for j, pb in enumerate(pbanks):
            op("dve", lambda e, j=j, pb=pb: e.scalar_tensor_tensor(out=rc["tmp"][:, j * 512:(j + 1) * 512], in0=PS[pb], scalar=ss[:, 5:6],
                                                                 in1=G[:, j * 512:(j + 1) * 512], op0=ALU.mult, op1=ALU.mult),
               reads=[r_ps[pb], r_ss, r_mod], writes=[rc["r_tmp"]])
        op("dve", lambda e: e.tensor_tensor(out=xt, in0=xt, in1=rc["tmp"], op=ALU.add), reads=[rc["r_tmp"], r_xt], writes=[r_xt])
        S.dma("sp", out[tt * P:(tt + 1) * P, :], xt, reads=[r_xt], writes=[r_out_tiles[tt]])
        if nctx is not None:
            norm_tile(nctx, xt, r_xt, nA, nsh, hdst, r_hdst, tt, npsb)

    def phase_ffn(l, hsrc, r_hsrc, r_out_tiles, last):
        m = ar.mark()
        TC = 512
        gT = ar.alloc([NFC, TC], BF16)
        r_gT = [Res() for _ in range(NFC)]
        NWU = 4
        wu = [ar.alloc([KT, 2, P], BF16) for _ in range(NWU)]
        r_wu = [Res() for _ in range(NWU)]
        wd = [ar.alloc([NFC, 512], BF16) for _ in range(2)]
        r_wd = [Res(), Res()]
        ca = [ar.alloc([TC], F32) for _ in range(2)]
        r_ca = [Res(), Res()]
        sa = ar.alloc([TC], F32)
        r_sa = Res()
        rc = make_resid_ctx()
        nctx = None if last else make_norm_ctx()
        fcw = colp[:, CP_FCW:CP_FCW + 132].rearrange("p (a b) -> p a b", a=44)
        fcb = colp[:, CP_FCB:CP_FCB + 44]
        wup_src = w_up[l].rearrange("(kt p) n -> p kt n", p=P)
        wdn_src = w_down[l].rearrange("(fc p) n -> p fc n", p=P)
        pwin = [psum[:, 0:1024], psum[:, 1024:2048], psum[:, 2048:3072]]
        r_pwin = [Res(), Res(), Res()]
        wi = 0
        pwi = 0
        for tc in range(SEQ // TC):
            c0 = tc * TC
            col0 = PAD + c0 - 1
            for fc in range(NFC):
                b = wi % NWU
                wi += 1
                wload(wu[b][:, :, 0, :], wup_src[:, :, fc * P:(fc + 1) * P], r_wu[b])
                wload(wu[b][:, :, 1, :], wup_src[:, :, D_FF + fc * P:D_FF + (fc + 1) * P], r_wu[b])
                for ab in range(2):
                    pw, r_pw = pwin[pwi % 3], r_pwin[pwi % 3]
                    pwi += 1
                    cidx = fc + 22 * ab
                    wins = [(0, 512), (512, 2)]
                    for wi_, (o, n) in enumerate(wins):
                        for kt in range(KT):
                            op("pe", lambda e, kt=kt, o=o, n=n, ab=ab, b=b, pw=pw: e.matmul(pw[:, o:o + n], lhsT=wu[b][:, kt, ab, :], rhs=hsrc[:, kt, col0 + o:col0 + o + n],
                                                                                      start=(kt == 0), stop=(kt == KT - 1)),
                               reads=[r_wu[b], r_hsrc], writes=[r_pw], inc=(kt == KT - 1 and wi_ == 1))
                    cab = ca[ab]
                    op("act", lambda e, pw=pw, cab=cab, cidx=cidx: e.activation(out=cab, in_=pw[:, 1:1 + TC], func=AF.Identity, scale=fcw[:, cidx, 1:2], bias=fcb[:, cidx:cidx + 1]),
                       reads=[r_pw, r_colp], writes=[r_ca[ab]])
                    op("dve", lambda e, pw=pw, cab=cab, cidx=cidx: e.scalar_tensor_tensor(out=cab, in0=pw[:, 0:TC], scalar=fcw[:, cidx, 0:1], in1=cab, op0=ALU.mult, op1=ALU.add),
                       reads=[r_pw, r_colp, r_ca[ab]], writes=[r_ca[ab]])
                    op("dve", lambda e, pw=pw, cab=cab, cidx=cidx: e.scalar_tensor_tensor(out=cab, in0=pw[:, 2:2 + TC], scalar=fcw[:, cidx, 2:3], in1=cab, op0=ALU.mult, op1=ALU.add),
                       reads=[r_pw, r_colp, r_ca[ab]], writes=[r_ca[ab]])
                op("act", lambda e: e.activation(out=sa, in_=ca[0], func=AF.Silu), reads=[r_ca[0]], writes=[r_sa])
                op("dve", lambda e, fc=fc: e.tensor_tensor(out=gT[:, fc, :], in0=sa, in1=ca[1], op=ALU.mult), reads=[r_sa, r_ca[1]], writes=[r_gT[fc]])
            for half in range(2):
                wload(wd[half], wdn_src[:, :, half * 512:(half + 1) * 512], r_wd[half])
            for t8 in range(TC // P):
                tt = tc * (TC // P) + t8
                for half in range(2):
                    pb = 6 + half
                    for fc in range(NFC):
                        op("pe", lambda e, fc=fc, half=half, pb=pb, t8=t8: e.matmul(PS[pb], lhsT=gT[:, fc, t8 * P:(t8 + 1) * P], rhs=wd[half][:, fc, :],
                                                                                  start=(fc == 0), stop=(fc == NFC - 1)),
                           reads=[r_gT[fc], r_wd[half]], writes=[r_ps[pb]], inc=(fc == NFC - 1))
                if last:
                    resid_tile(rc, [6, 7], mod[:, 5, :], out, tt, r_out_tiles)
                else:
                    resid_tile(rc, [6, 7], mod[:, 5, :], out, tt, r_out_tiles, nctx, mod_next[:, 1, :], mod_next[:, 0, :], hT[0], r_hT[0], 6)
        S.barrier()
        ar.release(m)

    r_out_tiles = [Res() for _ in range(TT)]
    mod_next = mod
    obr = nc.dram_tensor("obr", [3, 512, SEQ], BF16, kind="Internal").ap()
    r_obr = [Res() for _ in range(3)]
    win_src = lambda l: w_in[l].rearrange("(kt p) n -> p kt n", p=P)

    r_rope = Res()
    rope_holder = {}

    def build_rope():
        ropeC = ar.alloc([SEQ], F32)
        ropeS = ar.alloc([SEQ], F32)
        rope_holder["C"] = ropeC
        rope_holder["S"] = ropeS
        m = ar.mark()
        posi = ar.alloc([SEQ], I32)
        ang = ar.alloc([SEQ], F32)
        kf = ar.alloc([SEQ], F32)
        ki = ar.alloc([SEQ], I32)
        msk = ar.alloc([SEQ], F32)
        r_t = Res()
        TWO_PI = 6.283185307179586
        C1 = 6.28125
        C2 = TWO_PI - C1
        S.dma("sp", posi, pos_d.partition_broadcast(P), writes=[r_t])
        op("dve", lambda e: e.tensor_copy(out=ang, in_=posi), reads=[r_t], writes=[r_t])
        op("dve", lambda e: e.tensor_scalar(out=ang, in0=ang, scalar1=cst[:, 0:1], scalar2=None, op0=ALU.mult), reads=[r_t, r_const], writes=[r_t])

        def reduce_sin(dst, shift):
            op("dve", lambda e: e.tensor_scalar(out=kf, in0=ang, scalar1=shift, scalar2=1.0 / TWO_PI, op0=ALU.add, op1=ALU.mult), reads=[r_t], writes=[r_t])
            op("dve", lambda e: e.tensor_copy(out=ki, in_=kf), reads=[r_t], writes=[r_t])
            op("dve", lambda e: e.tensor_copy(out=kf, in_=ki), reads=[r_t], writes=[r_t])
            op("dve", lambda e: e.scalar_tensor_tensor(out=dst, in0=kf, scalar=-C1, in1=ang, op0=ALU.mult, op1=ALU.add), reads=[r_t], writes=[r_t])
            op("dve", lambda e: e.scalar_tensor_tensor(out=dst, in0=kf, scalar=-C2, in1=dst, op0=ALU.mult, op1=ALU.add), reads=[r_t], writes=[r_t])
            if shift != 0.0:
                op("dve", lambda e: e.tensor_scalar(out=dst, in0=dst, scalar1=shift, scalar2=None, op0=ALU.add), reads=[r_t], writes=[r_t])
            op("dve", lambda e: e.tensor_scalar(out=msk, in0=dst, scalar1=3.141592653589793, scalar2=-TWO_PI, op0=ALU.is_gt, op1=ALU.mult), reads=[r_t], writes=[r_t])
            op("dve", lambda e: e.tensor_tensor(out=dst, in0=dst, in1=msk, op=ALU.add), reads=[r_t], writes=[r_t])
            op("dve", lambda e: e.tensor_scalar(out=msk, in0=dst, scalar1=-3.141592653589793, scalar2=TWO_PI, op0=ALU.is_lt, op1=ALU.mult), reads=[r_t], writes=[r_t])
            op("dve", lambda e: e.tensor_tensor(out=dst, in0=dst, in1=msk, op=ALU.add), reads=[r_t], writes=[r_t])
            op("dve", lambda e: e.tensor_scalar(out=dst, in0=dst, scalar1=3.1415925, scalar2=-3.1415925, op0=ALU.min, op1=ALU.max), reads=[r_t], writes=[r_t])
            op("act", lambda e: e.activation(out=dst, in_=dst, func=AF.Sin), reads=[r_t], writes=[r_t])
        reduce_sin(ropeS, 0.0)
        reduce_sin(ropeC, 1.5707963267948966)
        op("dve", lambda e: e.tensor_scalar(out=ropeS, in0=ropeS, scalar1=cst[:, 1:2], scalar2=None, op0=ALU.mult), reads=[r_t, r_const], writes=[r_rope])
        S.barrier()
        ar.release(m)

    def phase_A(l):
        m = ar.mark()
        wa = ar.alloc([KT, 1024], BF16)
        r_wa = Res()
        wload(wa[:, :, 0:512], win_src(l)[:, :, C_AUV:C_AUV + 512], r_wa)
        wload(wa[:, :, 512:1024], win_src(l)[:, :, C_AUV + 512:C_AUV + 1024], r_wa)
        wsp_f = ar.alloc([4, P], F32)
        wsp = ar.alloc([4, P], BF16)
        r_wsp = Res()
        S.dma("sp", wsp_f, w_spT[l].rearrange("g q p -> q g p"), writes=[r_wsp])
        op("dve", lambda e: e.tensor_copy(out=wsp, in_=wsp_f), reads=[r_wsp], writes=[r_wsp])
        lnw = ar.alloc([1024], F32)
        r_ln = Res()
        S.dma("sp", lnw, rowp_d[l, :, RP_LNW:RP_LNW + 1024].partition_broadcast(P), writes=[r_ln])
        bsp = colp[:, CP_BSP:CP_BSP + 4]
        ug = [ar.alloc([512], F32) for _ in range(2)]
        vg = [ar.alloc([512], F32) for _ in range(2)]
        vn = [ar.alloc([512], BF16) for _ in range(2)]
        oa = [ar.alloc([512], BF16) for _ in range(2)]
        oaT = [ar.alloc([4, P], BF16) for _ in range(2)]
        st = [ar.alloc([8], F32) for _ in range(2)]
        junk = ar.alloc([512], BF16)
        r_b = [[Res() for _ in range(7)] for _ in range(2)]
        r_junk = Res()
        dstv = obr[0].rearrange("(wc p) t -> p wc t", p=P)
        for tt in range(TT):
            b = tt % 2
            r_ug, r_vg, r_vn, r_oa, r_oaT, r_st, _ = r_b[b]
            c0 = PAD + tt * P
            for hf_ in range(2):
                pb = 0 + hf_ + 2 * b
                for kt in range(KT):
                    op("pe", lambda e, kt=kt, hf_=hf_, pb=pb: e.matmul(PS[pb], lhsT=hT[0][:, kt, c0:c0 + P], rhs=wa[:, kt, hf_ * 512:(hf_ + 1) * 512],
                                                                  start=(kt == 0), stop=(kt == KT - 1)),
                       reads=[r_hT[0], r_wa], writes=[r_ps[pb]], inc=(kt == KT - 1))
            pu, pv = 0 + 2 * b, 1 + 2 * b
            op("act", lambda e: e.activation(out=ug[b], in_=PS[pu], func=AF.Gelu_apprx_tanh), reads=[r_ps[pu]], writes=[r_ug])
            op("act", lambda e: e.activation(out=vg[b], in_=PS[pv], func=AF.Gelu_apprx_tanh, accum_out=st[b][:, 0:1]), reads=[r_ps[pv]], writes=[r_vg, r_st])
            op("act", lambda e: e.activation(out=junk, in_=vg[b], func=AF.Square, accum_out=st[b][:, 1:2]), reads=[r_vg], writes=[r_junk, r_st])
            op("dve", lambda e: e.tensor_scalar(out=st[b][:, 2:3], in0=st[b][:, 0:1], scalar1=1.0 / 512, scalar2=None, op0=ALU.mult), reads=[r_st], writes=[r_st])
            op("dve", lambda e: e.tensor_tensor(out=st[b][:, 3:4], in0=st[b][:, 2:3], in1=st[b][:, 2:3], op=ALU.mult), reads=[r_st], writes=[r_st])
            op("dve", lambda e: e.scalar_tensor_tensor(out=st[b][:, 4:5], in0=st[b][:, 1:2], scalar=1.0 / 512, in1=st[b][:, 3:4], op0=ALU.mult, op1=ALU.subtract), reads=[r_st], writes=[r_st])
            op("dve", lambda e: e.tensor_scalar(out=st[b][:, 4:5], in0=st[b][:, 4:5], scalar1=LN_EPS, scalar2=None, op0=ALU.add), reads=[r_st], writes=[r_st])
            op("act", lambda e: e.activation(out=st[b][:, 5:6], in_=st[b][:, 4:5], func=AF.Sqrt), reads=[r_st], writes=[r_st])
            op("dve", lambda e: e.reciprocal(out=st[b][:, 6:7], in_=st[b][:, 5:6]), reads=[r_st], writes=[r_st])
            op("dve", lambda e: e.tensor_scalar(out=vg[b], in0=vg[b], scalar1=st[b][:, 2:3], scalar2=st[b][:, 6:7], op0=ALU.subtract, op1=ALU.mult), reads=[r_st, r_vg], writes=[r_vg])
            op("dve", lambda e: e.tensor_tensor(out=vg[b], in0=vg[b], in1=lnw[:, 0:512], op=ALU.mult), reads=[r_vg, r_ln], writes=[r_vg])
            op("dve", lambda e: e.tensor_tensor(out=vn[b], in0=vg[b], in1=lnw[:, 512:1024], op=ALU.add), reads=[r_vg, r_ln], writes=[r_vn])
            pm = 4 + b
            for g in range(4):
                op("pe", lambda e, g=g: e.matmul(PS[pm][:, g * P:(g + 1) * P], lhsT=wsp[:, g, :], rhs=vn[b][:, g * P:(g + 1) * P], start=True, stop=True),
                   reads=[r_wsp, r_vn], writes=[r_ps[pm]], inc=(g == 3))
            for g in range(4):
                op("dve", lambda e, g=g: e.scalar_tensor_tensor(out=oa[b][:, g * P:(g + 1) * P], in0=PS[pm][:, g * P:(g + 1) * P], scalar=bsp[:, g:g + 1],
                                                              in1=ug[b][:, g * P:(g + 1) * P], op0=ALU.add, op1=ALU.mult),
                   reads=[r_ps[pm], r_colp, r_ug], writes=[r_oa])
            pt = 6 + b
            for g in range(4):
                op("pe", lambda e, g=g: e.transpose(PSB[pt][:, g * P:(g + 1) * P], oa[b][:, g * P:(g + 1) * P], identb),
                   reads=[r_oa, r_const], writes=[r_ps[pt]], inc=(g == 3))
            op("act", lambda e: e.activation(out=oaT[b], in_=PSB[pt][:, 0:512].rearrange("p (a b) -> p a b", a=4), func=AF.Copy), reads=[r_ps[pt]], writes=[r_oaT])
            S.dma("sp", dstv[:, :, tt * P:(tt + 1) * P], oaT[b], reads=[r_oaT], writes=[r_obr[0]])
        S.barrier()
        ar.release(m)

    def zero_branch(n):
        m = ar.mark()
        z = ar.alloc([4, 512], BF16)
        r_z = Res()
        op("dve", lambda e: e.memset(z, 0.0), writes=[r_z])
        dstv = obr[n].rearrange("(wc p) t -> p wc t", p=P)
        for c in range(4):
            S.dma("sp", dstv[:, :, c * 512:(c + 1) * 512], z, reads=[r_z], writes=[r_obr[n]])
        S.barrier()
        ar.release(m)

    def phase_merge(l, src):
        m = ar.mark()
        mT = ar.alloc([KT, SEQ], BF16)
        r_mT = [Res() for _ in range(4)]
        m1 = ar.mark()
        oT = ar.alloc([3, 4, SEQ], BF16)
        r_oT = Res()
        wg = [ar.alloc([KT, 3, P], BF16) for _ in range(2)]
        r_wg = [Res(), Res()]
        wbr = [ar.alloc([3, 4, P], BF16) for _ in range(2)]
        r_wbr = [Res(), Res()]
        sg = [ar.alloc([512], F32) for _ in range(2)]
        r_sg = [Res(), Res()]
        macc = ar.alloc([512], F32)
        r_macc = Res()
        tmp = ar.alloc([512], F32)
        r_tmp = Res()
        bg = colp[:, CP_BGATE:CP_BGATE + 24]
        osrc = obr.rearrange("n (wc p) t -> p n wc t", p=P)
        for n in range(3):
            for tc in range(4):
                S.dma("sp", oT[:, n, :, tc * 512:(tc + 1) * 512], osrc[:, n, :, tc * 512:(tc + 1) * 512], reads=[r_obr[n]], writes=[r_oT])
        gi = 0
        for dc in range(KT):
            b = dc % 2
            for n in range(3):
                wload(wg[b][:, :, n, :], win_src(l)[:, :, C_GATE + n * D + dc * P:C_GATE + n * D + (dc + 1) * P], r_wg[b])
                wload(wbr[b][:, n, :, :], w_branch[l, n].rearrange("(wc p) d -> p wc d", p=P)[:, :, dc * P:(dc + 1) * P], r_wbr[b])
            for tc in range(4):
                c0 = PAD + tc * 512
                for n in range(3):
                    k = gi % 2
                    gi += 1
                    pg, pu = 0 + 2 * k, 1 + 2 * k
                    for kt in range(KT):
                        op("pe", lambda e, kt=kt, n=n, pg=pg: e.matmul(PS[pg], lhsT=wg[b][:, kt, n, :], rhs=hT[0][:, kt, c0:c0 + 512], start=(kt == 0), stop=(kt == KT - 1)),
                           reads=[r_wg[b], r_hT[0]], writes=[r_ps[pg]], inc=(kt == KT - 1))
                    for wc in range(4):
                        op("pe", lambda e, wc=wc, n=n, pu=pu: e.matmul(PS[pu], lhsT=wbr[b][:, n, wc, :], rhs=oT[:, n, wc, tc * 512:(tc + 1) * 512], start=(wc == 0), stop=(wc == 3)),
                           reads=[r_wbr[b], r_oT], writes=[r_ps[pu]], inc=(wc == 3))
                    op("act", lambda e, n=n, k=k, pg=pg: e.activation(out=sg[k], in_=PS[pg], func=AF.Sigmoid, bias=bg[:, n * 8 + dc:n * 8 + dc + 1]),
                       reads=[r_ps[pg], r_colp], writes=[r_sg[k]])
                    if n == 0:
                        op("dve", lambda e, k=k, pu=pu: e.tensor_tensor(out=macc, in0=PS[pu], in1=sg[k], op=ALU.mult), reads=[r_ps[pu], r_sg[k]], writes=[r_macc])
                    else:
                        op("dve", lambda e, k=k, pu=pu: e.tensor_tensor(out=tmp, in0=PS[pu], in1=sg[k], op=ALU.mult), reads=[r_ps[pu], r_sg[k]], writes=[r_tmp])
                        if n == 1:
                            op("dve", lambda e: e.tensor_tensor(out=macc, in0=macc, in1=tmp, op=ALU.add), reads=[r_tmp, r_macc], writes=[r_macc])
                        else:
                            op("dve", lambda e, dc=dc, tc=tc: e.tensor_tensor(out=mT[:, dc, tc * 512:(tc + 1) * 512], in0=macc, in1=tmp, op=ALU.add), reads=[r_tmp, r_macc], writes=[r_mT[tc]])
        S.barrier()
        ar.release(m1)
        wo = ar.alloc([KT, D], BF16)
        r_wo = Res()
        for hf_ in range(2):
            wload(wo[:, :, hf_ * 512:(hf_ + 1) * 512], w_o[l].rearrange("(kt p) n -> p kt n", p=P)[:, :, hf_ * 512:(hf_ + 1) * 512], r_wo)
        rc = make_resid_ctx()
        nctx = make_norm_ctx()
        for tt in range(TT):
            tc = tt // 4
            for hf_ in range(2):
                pb = 4 + hf_
                for dc in range(KT):
                    op("pe", lambda e, dc=dc, hf_=hf_, pb=pb: e.matmul(PS[pb], lhsT=mT[:, dc, tt * P:(tt + 1) * P], rhs=wo[:, dc, hf_ * 512:(hf_ + 1) * 512],
                                                                  start=(dc == 0), stop=(dc == KT - 1)),
                       reads=[r_mT[tc], r_wo], writes=[r_ps[pb]], inc=(dc == KT - 1))
            resid_tile(rc, [4, 5], mod[:, 2, :], src, tt, r_out_tiles, nctx, mod[:, 4, :], mod[:, 3, :], hT[1], r_hT[1], 6 + (tt % 2))
        S.barrier()
        ar.release(m)

    def phase_C(l):
        m = ar.mark()
        build_rope()
        ropeC, ropeS = rope_holder["C"], rope_holder["S"]
        SC = 192 ** -0.5
        qnw = colp[:, CP_QNW:CP_QNW + 3]
        kvnw = colp[:, CP_KVNW:CP_KVNW + 2]
        wkv = ar.alloc([KT, 256], BF16)
        wkr = ar.alloc([KT, 2, P], BF16)
        wcq = ar.alloc([KT, 384], BF16)
        wukv_k = ar.alloc([2, 4, P], BF16)
        wukv_v = ar.alloc([2, 4, P], BF16)
        wuq_n = ar.alloc([3, 4, P], BF16)
        wuq_p = ar.alloc([3, 2, 2, P], BF16)
        r_w = Res()
        ws = win_src(l)
        wload(wkv, ws[:, :, C_CKV:C_CKV + 256], r_w)
        wload(wcq, ws[:, :, C_CQ:C_CQ + 384], r_w)
        for hlf in range(2):
            wload(wkr[:, :, 0, hlf * 64:(hlf + 1) * 64], ws[:, :, C_KR:C_KR + 64], r_w)
            wload(wkr[:, :, 1, hlf * 64:(hlf + 1) * 64], w_krsw[l].rearrange("(kt p) n -> p kt n", p=P), r_w)
        ukv = w_ukv[l].rearrange("(c p) (h two e) -> p c h two e", p=P, h=4, two=2)
        for c in range(2):
            wload(wukv_k[:, c], ukv[:, c, :, 0, :], r_w)
            wload(wukv_v[:, c], ukv[:, c, :, 1, :], r_w)
        uq = w_uq[l].rearrange("(c p) (h e) -> p c h e", p=P, h=4)
        for c in range(3):
            wload(wuq_n[:, c], uq[:, c, :, 0:128], r_w)
        uqs = w_uqsw[l].rearrange("(c p) (h e) -> p c h e", p=P, h=4)
        for pr in range(2):
            for hh in range(2):
                wload(wuq_p[:, :, pr, 0, hh * 64:(hh + 1) * 64], uq[:, :, pr * 2 + hh, 128:192], r_w)
                wload(wuq_p[:, :, pr, 1, hh * 64:(hh + 1) * 64], uqs[:, :, pr * 2 + hh, :], r_w)
        knT = ar.alloc([4, SEQ], BF16)
        kpe = ar.alloc([SEQ], BF16)
        vx = ar.alloc([TT, 4, 132], BF16)
        r_kv = Res()
        op("dve", lambda e: e.memset(vx, 1.0), writes=[r_kv])
        cw = ar.alloc([3, 512], BF16)
        sq = ar.alloc([3, 512], BF16)
        r_cw = Res()
        rbc = ar.alloc([512], F32)
        r_rbc = Res()
        rcol = ar.alloc([8], F32)
        r_rcol = Res()
        t1 = ar.alloc([512], F32)
        t2 = ar.alloc([512], F32)
        r_t1 = Res()
        r_t2 = Res()

        def lat_chunk(wcols, nchunk, nw, tc, width):
            c0 = PAD + tc * 512
            for c in range(nchunk):
                pb = c % 2
                for kt in range(KT):
                    op("pe", lambda e, kt=kt, c=c, pb=pb: e.matmul(PS[pb], lhsT=wcols[:, kt, c * P:(c + 1) * P], rhs=hT[0][:, kt, c0:c0 + 512], start=(kt == 0), stop=(kt == KT - 1)),
                       reads=[r_w, r_hT[0]], writes=[r_ps[pb]], inc=(kt == KT - 1))
                op("act", lambda e, c=c, pb=pb: e.activation(out=cw[:, c, :], in_=PS[pb], func=AF.Copy, scale=nw[:, c:c + 1]), reads=[r_ps[pb], r_colp], writes=[r_cw])
                op("act", lambda e, c=c, pb=pb: e.activation(out=sq[:, c, :], in_=PS[pb], func=AF.Square), reads=[r_ps[pb]], writes=[r_cw])
            for c in range(nchunk):
                op("pe", lambda e, c=c: e.matmul(PS[2], lhsT=onesb, rhs=sq[:, c, :], start=(c == 0), stop=(c == nchunk - 1)),
                   reads=[r_cw, r_const], writes=[r_ps[2]], inc=(c == nchunk - 1))
            op("dve", lambda e: e.tensor_scalar(out=rbc, in0=PS[2], scalar1=1.0 / width, scalar2=RMS_EPS, op0=ALU.mult, op1=ALU.add), reads=[r_ps[2]], writes=[r_rbc])
            op("act", lambda e: e.activation(out=rbc, in_=rbc, func=AF.Sqrt), reads=[r_rbc], writes=[r_rbc])
            op("dve", lambda e: e.reciprocal(out=rbc, in_=rbc), reads=[r_rbc], writes=[r_rbc])

        def rope_evac(dst, pplain, psw, tc, scale_bc):
            tsl = slice(tc * 512, (tc + 1) * 512)
            op("dve", lambda e: e.tensor_tensor(out=t1, in0=PS[pplain], in1=ropeC[:, tsl], op=ALU.mult), reads=[r_ps[pplain], r_rope], writes=[r_t1])
            op("dve", lambda e: e.tensor_tensor(out=t2, in0=PS[psw], in1=ropeS[:, tsl], op=ALU.mult), reads=[r_ps[psw], r_rope], writes=[r_t2])
            if scale_bc is None:
                op("dve", lambda e: e.tensor_tensor(out=dst, in0=t1, in1=t2, op=ALU.add), reads=[r_t1, r_t2], writes=[r_kv])
            else:
                op("dve", lambda e: e.tensor_tensor(out=t1, in0=t1, in1=t2, op=ALU.add), reads=[r_t1, r_t2], writes=[r_t1])
                op("dve", lambda e: e.tensor_tensor(out=dst, in0=t1, in1=scale_bc, op=ALU.mult), reads=[r_t1, r_rbc], writes=[r_q])

        for tc in range(4):
            c0 = PAD + tc * 512
            lat_chunk(wkv, 2, kvnw, tc, 256.0)
            for t4 in range(4):
                for c in range(2):
                    op("pe", lambda e, c=c, t4=t4: e.matmul(PS[3][:, t4:t4 + 1], lhsT=sq[:, c, t4 * P:(t4 + 1) * P], rhs=onesb[:, 0:1], start=(c == 0), stop=(c == 1)),
                       reads=[r_cw, r_const], writes=[r_ps[3]], inc=(c == 1 and t4 == 3))
            op("dve", lambda e: e.tensor_scalar(out=rcol[:, 0:4], in0=PS[3][:, 0:4], scalar1=1.0 / 256, scalar2=RMS_EPS, op0=ALU.mult, op1=ALU.add), reads=[r_ps[3]], writes=[r_rcol])
            op("act", lambda e: e.activation(out=rcol[:, 0:4], in_=rcol[:, 0:4], func=AF.Sqrt), reads=[r_rcol], writes=[r_rcol])
            op("dve", lambda e: e.reciprocal(out=rcol[:, 4:8], in_=rcol[:, 0:4]), reads=[r_rcol], writes=[r_rcol])
            for h in range(4):
                pb = 4 + h % 2
                for c in range(2):
                    op("pe", lambda e, c=c, h=h, pb=pb: e.matmul(PS[pb], lhsT=wukv_k[:, c, h, :], rhs=cw[:, c, :], start=(c == 0), stop=(c == 1)),
                       reads=[r_w, r_cw], writes=[r_ps[pb]], inc=(c == 1))
                op("dve", lambda e, h=h, pb=pb: e.tensor_tensor(out=knT[:, h, tc * 512:(tc + 1) * 512], in0=PS[pb], in1=rbc, op=ALU.mult),
                   reads=[r_ps[pb], r_rbc], writes=[r_kv])
            for t4 in range(4):
                tt = tc * 4 + t4
                pb = 6 + t4 % 2
                for c in range(2):
                    op("pe", lambda e, c=c, t4=t4, pb=pb: e.matmul(PS[pb], lhsT=cw[:, c, t4 * P:(t4 + 1) * P], rhs=wukv_v[:, c, :, :], start=(c == 0), stop=(c == 1)),
                       reads=[r_w, r_cw], writes=[r_ps[pb]], inc=(c == 1))
                op("act", lambda e, tt=tt, t4=t4, pb=pb: e.activation(out=vx[:, tt, :, 0:P], in_=PS[pb].rearrange("p (h e) -> p h e", h=4), func=AF.Copy, scale=rcol[:, 4 + t4:5 + t4]),
                   reads=[r_ps[pb], r_rcol], writes=[r_kv])
            for j in range(2):
                for kt in range(KT):
                    op("pe", lambda e, kt=kt, j=j: e.matmul(PS[j], lhsT=wkr[:, kt, j, :], rhs=hT[0][:, kt, c0:c0 + 512], start=(kt == 0), stop=(kt == KT - 1)),
                       reads=[r_w, r_hT[0]], writes=[r_ps[j]], inc=(kt == KT - 1))
            r_q = r_kv
            rope_evac(kpe[:, tc * 512:(tc + 1) * 512], 0, 1, tc, None)

        qn = ar.alloc([4, 512], BF16)
        qp = ar.alloc([2, 512], BF16)
        r_q = Res()
        pT = [ar.alloc([512], BF16) for _ in range(3)]
        r_pT = [Res(), Res(), Res()]
        ei0 = [0]
        oc = ar.alloc([4, P], BF16)
        r_oc = Res()
        ocT = ar.alloc([4, 512], BF16)
        r_ocT = Res()
        rs = ar.alloc([4], F32)
        r_rs = Res()
        dstv = obr[2].rearrange("(h p) t -> p h t", p=P)
        ei = 0
        for tc in range(4):
            lat_chunk(wcq, 3, qnw, tc, 384.0)
            for h in range(4):
                pb = h % 2
                for c in range(3):
                    op("pe", lambda e, c=c, h=h, pb=pb: e.matmul(PS[pb], lhsT=wuq_n[:, c, h, :], rhs=cw[:, c, :], start=(c == 0), stop=(c == 2)),
                       reads=[r_w, r_cw], writes=[r_ps[pb]], inc=(c == 2))
                op("dve", lambda e, h=h, pb=pb: e.tensor_tensor(out=qn[:, h, :], in0=PS[pb], in1=rbc, op=ALU.mult), reads=[r_ps[pb], r_rbc], writes=[r_q])
            for pr in range(2):
                for j in range(2):
                    for c in range(3):
                        op("pe", lambda e, c=c, pr=pr, j=j: e.matmul(PS[j], lhsT=wuq_p[:, c, pr, j, :], rhs=cw[:, c, :], start=(c == 0), stop=(c == 2)),
                           reads=[r_w, r_cw], writes=[r_ps[j]], inc=(c == 2))
                rope_evac(qp[:, pr, :], 0, 1, tc, rbc)
            for h in range(4):
                pr, hh = h // 2, h % 2
                rows = slice(hh * 64, (hh + 1) * 64)

                def emit_qk(kt, h=h, pr=pr, rows=rows):
                    k = (ei0[0] + kt) % 3
                    ps_ = 1 + k
                    op("pe", lambda e: e.matmul(PS[ps_], lhsT=knT[:, h, kt * P:(kt + 1) * P], rhs=qn[:, h, :], start=True, stop=False),
                       reads=[r_kv, r_q], writes=[r_ps[ps_]], inc=False)
                    op("pe", lambda e: e.matmul(PS[ps_], lhsT=kpe[rows, kt * P:(kt + 1) * P], rhs=qp[rows, pr, :], start=False, stop=True),
                       reads=[r_kv, r_q], writes=[r_ps[ps_]], inc=True)
                    op("act", lambda e: e.activation(out=pT[k], in_=PS[ps_], func=AF.Exp, scale=SC), reads=[r_ps[ps_]], writes=[r_pT[k]])
                emit_qk(0)
                for kt in range(TT):
                    if kt + 1 < TT:
                        emit_qk(kt + 1)
                    k = (ei0[0] + kt) % 3
                    for qt in range(4):
                        op("pe", lambda e, qt=qt, kt=kt, k=k, h=h: e.matmul(PS[4 + qt][:, 0:129], lhsT=pT[k][:, qt * P:(qt + 1) * P], rhs=vx[:, kt, h, 0:129],
                                                                        start=(kt == 0), stop=(kt == TT - 1)),
                           reads=[r_pT[k], r_kv], writes=[r_ps[4 + qt]], inc=(kt == TT - 1 or qt == 3))
                ei0[0] += TT
                for qt in range(4):
                    op("dve", lambda e, qt=qt: e.reciprocal(out=rs[:, qt:qt + 1], in_=PS[4 + qt][:, 128:129]), reads=[r_ps[4 + qt]], writes=[r_rs])
                    op("dve", lambda e, qt=qt: e.tensor_scalar(out=oc[:, qt, :], in0=PS[4 + qt][:, 0:128], scalar1=rs[:, qt:qt + 1], scalar2=None, op0=ALU.mult),
                       reads=[r_ps[4 + qt], r_rs], writes=[r_oc])
                for qt in range(4):
                    op("pe", lambda e, qt=qt: e.transpose(PSB[0][:, qt * P:(qt + 1) * P], oc[:, qt, :], identb), reads=[r_oc, r_const], writes=[r_ps[0]], inc=(qt == 3))
                op("act", lambda e, h=h: e.activation(out=ocT[:, h, :], in_=PSB[0][:, 0:512], func=AF.Copy), reads=[r_ps[0]], writes=[r_ocT])
            S.dma("sp", dstv[:, :, tc * 512:(tc + 1) * 512], ocT, reads=[r_ocT], writes=[r_obr[2]])
        S.barrier()
        ar.release(m)

    ofw = nc.dram_tensor("ofw", [SEQ, 512], F32, kind="Internal").ap()
    r_ofw = [Res() for _ in range(TT)]
    AX = mybir.AxisListType

    def phase_B(l):
        m = ar.mark()
        ws = win_src(l)
        qkvT = ar.alloc([12, SEQ], BF16)
        r_qkv = Res()
        gb = ar.alloc([TT, 16], F32)
        r_gb = Res()
        dcw = colp[:, CP_DNCONV:CP_DNCONV + 60].rearrange("p (a b) -> p a b", a=12)
        m1 = ar.mark()
        wq = [ar.alloc([KT, P], BF16) for _ in range(2)]
        r_wq = [Res(), Res()]
        acc = ar.alloc([512], F32)
        r_acc = Res()
        sil = ar.alloc([512], F32)
        r_sil = Res()
        sqb = ar.alloc([512], BF16)
        r_sqb = Res()
        rst = ar.alloc([512], F32)
        r_rst = Res()
        wi = 0
        for cc in range(12):
            b = wi % 2
            wi += 1
            wload(wq[b], ws[:, :, C_QKV + cc * P:C_QKV + (cc + 1) * P], r_wq[b])
            for tc in range(4):
                col0 = PAD + tc * 512 - 2
                k = (cc * 4 + tc) % 2
                pw = psum[:, k * 1024:(k + 1) * 1024]
                r_pw_ = [r_ps[2 * k], r_ps[2 * k + 1]]
                for wi_, (o, n) in enumerate([(0, 512), (512, 4)]):
                    for kt in range(KT):
                        op("pe", lambda e, kt=kt, o=o, n=n, pw=pw: e.matmul(pw[:, o:o + n], lhsT=wq[b][:, kt, :], rhs=hT[0][:, kt, col0 + o:col0 + o + n], start=(kt == 0), stop=(kt == KT - 1)),
                           reads=[r_wq[b], r_hT[0]], writes=r_pw_, inc=(kt == KT - 1 and wi_ == 1))
                op("act", lambda e, pw=pw: e.activation(out=acc, in_=pw[:, 2:514], func=AF.Copy, scale=dcw[:, cc, 2:3]), reads=r_pw_ + [r_colp], writes=[r_acc])
                for kk_ in (0, 1, 3, 4):
                    op("dve", lambda e, kk_=kk_, pw=pw: e.scalar_tensor_tensor(out=acc, in0=pw[:, kk_:kk_ + 512], scalar=dcw[:, cc, kk_:kk_ + 1], in1=acc, op0=ALU.mult, op1=ALU.add),
                       reads=r_pw_ + [r_colp, r_acc], writes=[r_acc])
                dst = qkvT[:, cc, tc * 512:(tc + 1) * 512]
                if cc >= 8:
                    op("act", lambda e, dst=dst: e.activation(out=dst, in_=acc, func=AF.Silu), reads=[r_acc], writes=[r_qkv])
                else:
                    op("act", lambda e: e.activation(out=sil, in_=acc, func=AF.Silu), reads=[r_acc], writes=[r_sil])
                    op("act", lambda e: e.activation(out=sqb, in_=sil, func=AF.Square), reads=[r_sil], writes=[r_sqb])
                    op("pe", lambda e: e.matmul(PS[4], lhsT=onesb, rhs=sqb, start=True, stop=True), reads=[r_sqb, r_const], writes=[r_ps[4]])
                    mul = 128.0 if cc < 4 else 1.0
                    op("dve", lambda e, mul=mul: e.tensor_scalar(out=rst, in0=PS[4], scalar1=mul, scalar2=mul * 1e-6, op0=ALU.mult, op1=ALU.add), reads=[r_ps[4]], writes=[r_rst])
                    op("act", lambda e: e.activation(out=rst, in_=rst, func=AF.Sqrt), reads=[r_rst], writes=[r_rst])
                    op("dve", lambda e: e.reciprocal(out=rst, in_=rst), reads=[r_rst], writes=[r_rst])
                    op("dve", lambda e, dst=dst: e.tensor_tensor(out=dst, in0=sil, in1=rst, op=ALU.mult), reads=[r_sil, r_rst], writes=[r_qkv])
        S.barrier()
        ar.release(m1)
        wz = ar.alloc([KT, 528], BF16)
        r_wz = Res()
        wload(wz[:, :, 0:512], ws[:, :, C_Z:C_Z + 512], r_wz)
        wload(wz[:, :, 512:528], ws[:, :, C_BETA:C_BETA + 16], r_wz)
        abias = ar.alloc([16], F32)
        r_ab = Res()
        S.dma("sp", abias, rowp_d[l, :, RP_ALOG:RP_ALOG + 16].partition_broadcast(P), writes=[r_ab])
        op("act", lambda e: e.activation(out=abias[:, 0:8], in_=abias[:, 0:8], func=AF.Exp), reads=[r_ab], writes=[r_ab])
        op("dve", lambda e: e.tensor_scalar(out=abias[:, 0:8], in0=abias[:, 0:8], scalar1=-1.0, scalar2=None, op0=ALU.mult), reads=[r_ab], writes=[r_ab])
        onw = ar.alloc([P], F32)
        r_onw = Res()
        S.dma("sp", onw, rowp_d[l, :, RP_ONW:RP_ONW + P].partition_broadcast(P), writes=[r_onw])
        tmp8 = ar.alloc([32], F32)
        r_t8 = Res()
        for tt in range(TT):
            c0 = PAD + tt * P
            for kt in range(KT):
                op("pe", lambda e, kt=kt: e.matmul(PS[5][:, 0:16], lhsT=hT[0][:, kt, c0:c0 + P], rhs=wz[:, kt, 512:528], start=(kt == 0), stop=(kt == KT - 1)),
                   reads=[r_wz, r_hT[0]], writes=[r_ps[5]], inc=(kt == KT - 1))
            op("act", lambda e, tt=tt: e.activation(out=gb[:, tt, 0:8], in_=PS[5][:, 0:8], func=AF.Sigmoid), reads=[r_ps[5]], writes=[r_gb])
            op("dve", lambda e: e.tensor_tensor(out=tmp8[:, 0:8], in0=PS[5][:, 8:16], in1=abias[:, 8:16], op=ALU.add), reads=[r_ps[5], r_ab], writes=[r_t8])
            op("act", lambda e: e.activation(out=tmp8[:, 8:16], in_=tmp8[:, 0:8], func=AF.Abs), reads=[r_t8], writes=[r_t8])
            op("act", lambda e: e.activation(out=tmp8[:, 8:16], in_=tmp8[:, 8:16], func=AF.Exp, scale=-1.0), reads=[r_t8], writes=[r_t8])
            op("act", lambda e: e.activation(out=tmp8[:, 8:16], in_=tmp8[:, 8:16], func=AF.Ln, bias=1.0), reads=[r_t8], writes=[r_t8])
            op("dve", lambda e: e.scalar_tensor_tensor(out=tmp8[:, 16:24], in0=tmp8[:, 0:8], scalar=0.0, in1=tmp8[:, 8:16], op0=ALU.max, op1=ALU.add), reads=[r_t8], writes=[r_t8])
            op("dve", lambda e, tt=tt: e.tensor_tensor(out=gb[:, tt, 8:16], in0=tmp8[:, 16:24], in1=abias[:, 0:8], op=ALU.mult), reads=[r_t8, r_ab], writes=[r_gb])
        onesf = ar.alloc([P], F32)
        negf = ar.alloc([P], F32)
        incl = [ar.alloc([P], F32) for _ in range(2)]
        negm = [ar.alloc([P], F32) for _ in range(2)]
        strict = [ar.alloc([P], F32) for _ in range(2)]
        r_mk = Res()
        op("pool", lambda e: e.memset(onesf, 1.0), writes=[r_mk])
        op("pool", lambda e: e.memset(negf, -1.0), writes=[r_mk])
        op("pool", lambda e: e.affine_select(out=incl[0], in_=onesf, pattern=[[-1, P]], compare_op=ALU.is_ge, fill=0.0, base=0, channel_multiplier=1), reads=[r_mk], writes=[r_mk])
        op("pool", lambda e: e.affine_select(out=incl[1], in_=onesf, pattern=[[1, P]], compare_op=ALU.is_ge, fill=0.0, base=0, channel_multiplier=-1), reads=[r_mk], writes=[r_mk])
        for d_ in range(2):
            op("dve", lambda e, d_=d_: e.tensor_scalar(out=negm[d_], in0=incl[d_], scalar1=-1.0, scalar2=1e30, op0=ALU.add, op1=ALU.mult), reads=[r_mk], writes=[r_mk])
            op("dve", lambda e, d_=d_: e.tensor_tensor(out=strict[d_], in0=incl[d_], in1=identf, op=ALU.subtract), reads=[r_mk, r_const], writes=[r_mk])
        H4 = [4, P]
        sc = ar.alloc([24], F32); r_sc = Res()
        gtri = ar.alloc(H4, F32); r_gtri = Res()
        Dm = ar.alloc(H4, F32); r_Dm = Res()
        Dms = ar.alloc(H4, F32); r_Dms = Res()
        At = ar.alloc(H4, BF16); r_At = Res()
        Ufb = ar.alloc(H4, BF16); r_Ufb = Res()
        ar2 = Arena.__new__(Arena)
        ar2.t, ar2.off, ar2.nbytes = ar.t, hT_off[1], hT_off[1] + KT * HTW * 2
        L32 = ar2.alloc(H4, F32); r_L32 = Res()
        M32 = ar2.alloc(H4, F32); r_M32 = Res()
        LP = [ar2.alloc(H4, F32) for _ in range(2)]; r_LP = [Res(), Res()]
        MP = [ar2.alloc(H4, F32) for _ in range(2)]; r_MP = [Res(), Res()]
        Tm = [ar2.alloc(H4, F32) for _ in range(2)]; r_Tm = [Res(), Res()]
        Um = [ar2.alloc(H4, F32) for _ in range(2)]; r_Um = [Res(), Res()]
        Lo1 = ar2.alloc(H4, F32); Mo1 = ar2.alloc(H4, F32); Lo2 = ar2.alloc(H4, F32); r_Lo = Res()
        Xa = ar2.alloc(H4, F32); r_Xa = Res()
        Xb = ar2.alloc(H4, F32); r_Xb = Res()
        BD, OF1, OF2 = cst[:, 4:132], cst[:, 132:260], cst[:, 260:388]
        AtT = ar.alloc(H4, BF16); r_AtT = Res()
        kbg = ar.alloc(H4, BF16); kd = ar.alloc(H4, BF16); vb = ar.alloc(H4, BF16); r_kv3 = Res()
        u4 = ar.alloc(H4, F32); r_u4 = Res()
        wT4 = ar.alloc(H4, BF16); r_wT = Res()
        vn4 = ar.alloc(H4, BF16); r_vn = Res()
        t4 = ar.alloc(H4, F32); r_t4 = Res()
        och = ar.alloc(H4, F32); r_och = Res()
        S4 = ar.alloc(H4, F32); Sb4 = ar.alloc(H4, BF16); r_S = Res(); r_Sb = Res()
        ofl = ar.alloc([512], F32); r_ofl = Res()
        ss4 = ar.alloc([12], F32); r_ss4 = Res()
        szb = ar.alloc([512], F32); r_sz = Res()
        obb = ar.alloc([512], BF16); r_obb = Res()
        obT = ar.alloc(H4, BF16); r_obT = Res()
        bank = [0]

        def nb():
            bank[0] = (bank[0] + 1) % 8
            return bank[0]
        bc_h = lambda a: a.unsqueeze(1).to_broadcast([P, 4, P])
        bc_e = lambda a: a.unsqueeze(2).to_broadcast([P, 4, P])
        v4 = lambda pb: PS[pb].rearrange("p (h e) -> p h e", h=4)
        v4b = lambda pb: PSB[pb][:, 0:512].rearrange("p (h e) -> p h e", h=4)
        dstv = obr[1].rearrange("(h p) t -> p h t", p=P)
        for d_ in range(2):
            op("dve", lambda e: e.memset(S4, 0.0), writes=[r_S])
            op("dve", lambda e: e.memset(Sb4, 0.0), writes=[r_Sb])
            tric = incl[1 - d_]
            for step in range(TT):
                c = step if d_ == 0 else TT - 1 - step
                tsl = slice(c * P, (c + 1) * P)
                qTh = lambda h: qkvT[:, h, tsl]
                kTh = lambda h: qkvT[:, 4 + h, tsl]
                vTh = lambda h: qkvT[:, 8 + h, tsl]
                beta4 = gb[:, c, d_ * 4:d_ * 4 + 4]
                g4 = gb[:, c, 8 + d_ * 4:8 + d_ * 4 + 4]
                p1 = nb()
                op("pe", lambda e: e.matmul(PS[p1][:, 0:4], lhsT=tric, rhs=g4, start=True, stop=True), reads=[r_mk, r_gb], writes=[r_ps[p1]], inc=False)
                op("pe", lambda e: e.matmul(PS[p1][:, 4:8], lhsT=onesf, rhs=g4, start=True, stop=True), reads=[r_mk, r_gb], writes=[r_ps[p1]])
                op("dve", lambda e: e.tensor_copy(out=sc[:, 0:8], in_=PS[p1][:, 0:8]), reads=[r_ps[p1]], writes=[r_sc])
                op("act", lambda e: e.activation(out=sc[:, 8:12], in_=sc[:, 0:4], func=AF.Exp), reads=[r_sc], writes=[r_sc])
                op("dve", lambda e: e.tensor_tensor(out=sc[:, 12:16], in0=sc[:, 4:8], in1=sc[:, 0:4], op=ALU.subtract), reads=[r_sc], writes=[r_sc])
                op("act", lambda e: e.activation(out=sc[:, 12:16], in_=sc[:, 12:16], func=AF.Exp), reads=[r_sc], writes=[r_sc])
                op("act", lambda e: e.activation(out=sc[:, 16:20], in_=sc[:, 4:8], func=AF.Exp), reads=[r_sc], writes=[r_sc])
                op("dve", lambda e: e.tensor_tensor(out=sc[:, 20:24], in0=sc[:, 8:12], in1=beta4, op=ALU.mult), reads=[r_sc, r_gb], writes=[r_sc])
                eg, ekd, etot, bge = sc[:, 8:12], sc[:, 12:16], sc[:, 16:20], sc[:, 20:24]
                D0 = (d_ == 0 and step == 0)
                if D0:
                    dbg("gb0", gb[:, 0, :], r_gb, 16)
                    dbg("sc", sc, r_sc, 24)
                    dbg("qT", qkvT[:, 0, 0:512], r_qkv, 512)
                    dbg("kT", qkvT[:, 4, 0:512], r_qkv, 512)
                    dbg("vT", qkvT[:, 8, 0:512], r_qkv, 512)
                for h in range(4):
                    op("dve", lambda e, h=h: e.tensor_scalar(out=gtri[:, h, :], in0=tric, scalar1=g4[:, h:h + 1], scalar2=None, op0=ALU.mult), reads=[r_mk, r_gb], writes=[r_gtri])
                p2 = nb()
                for h in range(4):
                    op("pe", lambda e, h=h: e.matmul(PS[p2][:, h * P:(h + 1) * P], lhsT=gtri[:, h, :], rhs=onesf, start=True, stop=False), reads=[r_gtri, r_mk], writes=[r_ps[p2]], inc=False)
                    op("pe", lambda e, h=h: e.matmul(PS[p2][:, h * P:(h + 1) * P], lhsT=negf, rhs=gtri[:, h, :], start=False, stop=True), reads=[r_gtri, r_mk], writes=[r_ps[p2]], inc=(h == 3))
                op("dve", lambda e: e.tensor_tensor(out=Dm, in0=v4(p2), in1=bc_h(negm[d_]), op=ALU.add), reads=[r_ps[p2], r_mk], writes=[r_Dm])
                op("act", lambda e: e.activation(out=Dm, in_=Dm, func=AF.Exp), reads=[r_Dm], writes=[r_Dm])
                op("dve", lambda e: e.tensor_tensor(out=Dms, in0=Dm, in1=bc_h(strict[d_]), op=ALU.mult), reads=[r_Dm, r_mk], writes=[r_Dms])
                if D0:
                    dbg("Dm", Dm.rearrange("p h e -> p (h e)"), r_Dm, 512)
                p3, p4 = nb(), nb()
                for h in range(4):
                    op("pe", lambda e, h=h: e.matmul(PS[p3][:, h * P:(h + 1) * P], lhsT=kTh(h), rhs=kTh(h), start=True, stop=True), reads=[r_qkv], writes=[r_ps[p3]], inc=(h == 3))
                for h in range(4):
                    op("pe", lambda e, h=h: e.matmul(PS[p4][:, h * P:(h + 1) * P], lhsT=qTh(h), rhs=kTh(h), start=True, stop=True), reads=[r_qkv], writes=[r_ps[p4]], inc=(h == 3))
                for h in range(4):
                    op("dve", lambda e, h=h: e.scalar_tensor_tensor(out=L32[:, h, :], in0=PS[p3][:, h * P:(h + 1) * P], scalar=beta4[:, h:h + 1], in1=Dms[:, h, :], op0=ALU.mult, op1=ALU.mult),
                       reads=[r_ps[p3], r_gb, r_Dms], writes=[r_L32])
                op("dve", lambda e: e.tensor_tensor(out=At, in0=v4(p4), in1=Dm, op=ALU.mult), reads=[r_ps[p4], r_Dm], writes=[r_At])
                p5, p6 = nb(), nb()
                for h in range(4):
                    op("pe", lambda e, h=h: e.transpose(PS[p5][:, h * P:(h + 1) * P], L32[:, h, :], identf), reads=[r_L32, r_const], writes=[r_ps[p5]], inc=(h == 3))
                for h in range(4):
                    op("pe", lambda e, h=h: e.transpose(PSB[p6][:, h * P:(h + 1) * P], At[:, h, :], identb), reads=[r_At, r_const], writes=[r_ps[p6]], inc=(h == 3))
                op("act", lambda e: e.activation(out=M32, in_=v4(p5), func=AF.Copy), reads=[r_ps[p5]], writes=[r_M32])
                op("act", lambda e: e.activation(out=AtT, in_=v4b(p6), func=AF.Copy), reads=[r_ps[p6]], writes=[r_AtT])
                op("dve", lambda e: e.tensor_tensor(out=LP[0], in0=L32, in1=bc_h(BD), op=ALU.mult), reads=[r_L32, r_const], writes=[r_LP[0]])
                op("dve", lambda e: e.tensor_tensor(out=MP[0], in0=M32, in1=bc_h(BD), op=ALU.mult), reads=[r_M32, r_const], writes=[r_MP[0]])
                op("dve", lambda e: e.tensor_tensor(out=Lo1, in0=L32, in1=bc_h(OF1), op=ALU.mult), reads=[r_L32, r_const], writes=[r_Lo])
                op("dve", lambda e: e.tensor_tensor(out=Mo1, in0=M32, in1=bc_h(OF1), op=ALU.mult), reads=[r_M32, r_const], writes=[r_Lo])
                op("dve", lambda e: e.tensor_tensor(out=Lo2, in0=L32, in1=bc_h(OF2), op=ALU.mult), reads=[r_L32, r_const], writes=[r_Lo])
                op("dve", lambda e: e.tensor_tensor(out=Tm[0], in0=bc_h(identf), in1=LP[0], op=ALU.subtract), reads=[r_const, r_LP[0]], writes=[r_Tm[0]])
                op("dve", lambda e: e.tensor_tensor(out=Um[0], in0=bc_h(identf), in1=MP[0], op=ALU.subtract), reads=[r_const, r_MP[0]], writes=[r_Um[0]])

                def mm4(pb, lhs, r_lhs, rhs, r_rhs):
                    for h in range(4):
                        op("pe", lambda e, h=h: e.matmul(PS[pb][:, h * P:(h + 1) * P], lhsT=lhs[:, h, :], rhs=rhs[:, h, :], start=True, stop=True),
                           reads=[r_lhs, r_rhs], writes=[r_ps[pb]], inc=(h == 3))
                cur = 0
                tcur = 0
                for lev in range(1, 5):
                    nxt = 1 - cur
                    pL, pM = nb(), nb()
                    mm4(pL, MP[cur], r_MP[cur], LP[cur], r_LP[cur])
                    mm4(pM, LP[cur], r_LP[cur], MP[cur], r_MP[cur])
                    op("act", lambda e: e.activation(out=LP[nxt], in_=v4(pL), func=AF.Copy), reads=[r_ps[pL]], writes=[r_LP[nxt]])
                    op("dve", lambda e: e.tensor_copy(out=MP[nxt], in_=v4(pM)), reads=[r_ps[pM]], writes=[r_MP[nxt]])
                    pT, pU = nb(), nb()
                    mm4(pT, MP[nxt], r_MP[nxt], Tm[tcur], r_Tm[tcur])
                    mm4(pU, LP[nxt], r_LP[nxt], Um[tcur], r_Um[tcur])
                    op("dve", lambda e: e.tensor_tensor(out=Tm[1 - tcur], in0=v4(pT), in1=Tm[tcur], op=ALU.add), reads=[r_ps[pT], r_Tm[tcur]], writes=[r_Tm[1 - tcur]])
                    op("dve", lambda e: e.tensor_tensor(out=Um[1 - tcur], in0=v4(pU), in1=Um[tcur], op=ALU.add), reads=[r_ps[pU], r_Um[tcur]], writes=[r_Um[1 - tcur]])
                    tcur = 1 - tcur
                    cur = nxt
                Td, r_Td, Ud, r_Ud = Tm[tcur], r_Tm[tcur], Um[tcur], r_Um[tcur]
                T64, r_T64, U64, r_U64 = Tm[1 - tcur], r_Tm[1 - tcur], Um[1 - tcur], r_Um[1 - tcur]
                pX, pXp = nb(), nb()
                mm4(pX, Mo1, r_Lo, Td, r_Td)
                mm4(pXp, Lo1, r_Lo, Ud, r_Ud)
                op("act", lambda e: e.activation(out=Xa, in_=v4(pX), func=AF.Copy), reads=[r_ps[pX]], writes=[r_Xa])
                op("dve", lambda e: e.tensor_copy(out=Xb, in_=v4(pXp)), reads=[r_ps[pXp]], writes=[r_Xb])
                pY, pYp = nb(), nb()
                mm4(pY, Ud, r_Ud, Xa, r_Xa)
                mm4(pYp, Td, r_Td, Xb, r_Xb)
                op("dve", lambda e: e.tensor_tensor(out=T64, in0=Td, in1=v4(pY), op=ALU.subtract), reads=[r_ps[pY], r_Td], writes=[r_T64])
                op("dve", lambda e: e.tensor_tensor(out=U64, in0=Ud, in1=v4(pYp), op=ALU.subtract), reads=[r_ps[pYp], r_Ud], writes=[r_U64])
                pXp = nb()
                mm4(pXp, Lo2, r_Lo, U64, r_U64)
                op("act", lambda e: e.activation(out=Xb, in_=v4(pXp), func=AF.Copy), reads=[r_ps[pXp]], writes=[r_Xb])
                pYp = nb()
                mm4(pYp, T64, r_T64, Xb, r_Xb)
                op("dve", lambda e: e.tensor_tensor(out=Ufb, in0=U64, in1=v4(pYp), op=ALU.subtract), reads=[r_ps[pYp], r_U64], writes=[r_Ufb])
                Uf, r_Uf = Ufb, r_Ufb
                if D0:
                    dbg("U", Uf.rearrange("p h e -> p (h e)"), r_Uf, 512)
                p7, p8 = nb(), nb()
                for h in range(4):
                    op("pe", lambda e, h=h: e.transpose(PSB[p7][:, h * P:(h + 1) * P], kTh(h), identb), reads=[r_qkv, r_const], writes=[r_ps[p7]], inc=(h == 3))
                for h in range(4):
                    op("pe", lambda e, h=h: e.transpose(PSB[p8][:, h * P:(h + 1) * P], vTh(h), identb), reads=[r_qkv, r_const], writes=[r_ps[p8]], inc=(h == 3))
                op("dve", lambda e: e.tensor_tensor(out=kbg, in0=v4b(p7), in1=bc_e(bge), op=ALU.mult), reads=[r_ps[p7], r_sc], writes=[r_kv3])
                op("dve", lambda e: e.tensor_tensor(out=kd, in0=v4b(p7), in1=bc_e(ekd), op=ALU.mult), reads=[r_ps[p7], r_sc], writes=[r_kv3])
                op("dve", lambda e: e.tensor_tensor(out=vb, in0=v4b(p8), in1=bc_e(beta4), op=ALU.mult), reads=[r_ps[p8], r_gb], writes=[r_kv3])
                p9, p10 = nb(), nb()
                for h in range(4):
                    op("pe", lambda e, h=h: e.matmul(PS[p9][:, h * P:(h + 1) * P], lhsT=Uf[:, h, :], rhs=vb[:, h, :], start=True, stop=True), reads=[r_Uf, r_kv3], writes=[r_ps[p9]], inc=(h == 3))
                for h in range(4):
                    op("pe", lambda e, h=h: e.matmul(PS[p10][:, h * P:(h + 1) * P], lhsT=kbg[:, h, :], rhs=Uf[:, h, :], start=True, stop=True), reads=[r_Uf, r_kv3], writes=[r_ps[p10]], inc=(h == 3))
                op("act", lambda e: e.activation(out=u4, in_=v4(p9), func=AF.Copy), reads=[r_ps[p9]], writes=[r_u4])
                op("act", lambda e: e.activation(out=wT4, in_=v4(p10), func=AF.Copy), reads=[r_ps[p10]], writes=[r_wT])
                p11 = nb()
                for h in range(4):
                    op("pe", lambda e, h=h: e.matmul(PS[p11][:, h * P:(h + 1) * P], lhsT=wT4[:, h, :], rhs=Sb4[:, h, :], start=True, stop=True), reads=[r_wT, r_Sb], writes=[r_ps[p11]], inc=(h == 3))
                op("dve", lambda e: e.tensor_tensor(out=vn4, in0=u4, in1=v4(p11), op=ALU.subtract), reads=[r_u4, r_ps[p11]], writes=[r_vn])
                p12, p13, p14 = nb(), nb(), nb()
                for h in range(4):
                    op("pe", lambda e, h=h: e.matmul(PS[p12][:, h * P:(h + 1) * P], lhsT=qTh(h), rhs=Sb4[:, h, :], start=True, stop=True), reads=[r_qkv, r_Sb], writes=[r_ps[p12]], inc=(h == 3))
                for h in range(4):
                    op("pe", lambda e, h=h: e.matmul(PS[p13][:, h * P:(h + 1) * P], lhsT=AtT[:, h, :], rhs=vn4[:, h, :], start=True, stop=True), reads=[r_AtT, r_vn], writes=[r_ps[p13]], inc=(h == 3))
                for h in range(4):
                    op("pe", lambda e, h=h: e.matmul(PS[p14][:, h * P:(h + 1) * P], lhsT=kd[:, h, :], rhs=vn4[:, h, :], start=True, stop=True), reads=[r_kv3, r_vn], writes=[r_ps[p14]], inc=(h == 3))
                op("dve", lambda e: e.tensor_tensor(out=t4, in0=v4(p12), in1=bc_e(eg), op=ALU.mult), reads=[r_ps[p12], r_sc], writes=[r_t4])
                op("dve", lambda e: e.tensor_tensor(out=och, in0=v4(p13), in1=t4, op=ALU.add), reads=[r_ps[p13], r_t4], writes=[r_och])
                op("dve", lambda e: e.tensor_tensor(out=S4, in0=S4, in1=bc_e(etot), op=ALU.mult), reads=[r_sc, r_S], writes=[r_S])
                op("dve", lambda e: e.tensor_tensor(out=S4, in0=v4(p14), in1=S4, op=ALU.add), reads=[r_ps[p14], r_S], writes=[r_S])
                op("act", lambda e: e.activation(out=Sb4, in_=S4, func=AF.Copy), reads=[r_S], writes=[r_Sb])
                ochf = och.rearrange("p h e -> p (h e)")
                if D0:
                    dbg("u4", u4.rearrange("p h e -> p (h e)"), r_u4, 512)
                    dbg("wT", wT4.rearrange("p h e -> p (h e)"), r_wT, 512)
                    dbg("vn", vn4.rearrange("p h e -> p (h e)"), r_vn, 512)
                    dbg("och", ochf, r_och, 512)
                    dbg("S4", S4.rearrange("p h e -> p (h e)"), r_S, 512)
                if d_ == 0 and step == 1:
                    dbg("och1", ochf, r_och, 512)
                if d_ == 1 and step == 0:
                    dbg("ochb", ochf, r_och, 512)
                if d_ == 0:
                    S.dma("sp", ofw[c * P:(c + 1) * P, :], ochf, reads=[r_och], writes=[r_ofw[c]])
                else:
                    S.dma("sp", ofl, ofw[c * P:(c + 1) * P, :], reads=[r_ofw[c]], writes=[r_ofl])
                    op("dve", lambda e: e.tensor_tensor(out=ochf, in0=ochf, in1=ofl, op=ALU.add), reads=[r_och, r_ofl], writes=[r_och])
                    op("dve", lambda e: e.tensor_tensor(out=t4, in0=och, in1=och, op=ALU.mult), reads=[r_och], writes=[r_t4])
                    op("dve", lambda e: e.reduce_sum(out=ss4[:, 0:4], in_=t4, axis=AX.X), reads=[r_t4], writes=[r_ss4])
                    op("dve", lambda e: e.tensor_scalar(out=ss4[:, 4:8], in0=ss4[:, 0:4], scalar1=1.0 / P, scalar2=RMS_EPS, op0=ALU.mult, op1=ALU.add), reads=[r_ss4], writes=[r_ss4])
                    op("act", lambda e: e.activation(out=ss4[:, 4:8], in_=ss4[:, 4:8], func=AF.Sqrt), reads=[r_ss4], writes=[r_ss4])
                    op("dve", lambda e: e.reciprocal(out=ss4[:, 8:12], in_=ss4[:, 4:8]), reads=[r_ss4], writes=[r_ss4])
                    op("dve", lambda e: e.tensor_tensor(out=t4, in0=och, in1=bc_e(ss4[:, 8:12]), op=ALU.mult), reads=[r_och, r_ss4], writes=[r_t4])
                    op("dve", lambda e: e.tensor_tensor(out=t4, in0=t4, in1=bc_h(onw), op=ALU.mult), reads=[r_t4, r_onw], writes=[r_t4])
                    pz = nb()
                    c0 = PAD + c * P
                    for kt in range(KT):
                        op("pe", lambda e, kt=kt: e.matmul(PS[pz], lhsT=hT[0][:, kt, c0:c0 + P], rhs=wz[:, kt, 0:512], start=(kt == 0), stop=(kt == KT - 1)),
                           reads=[r_wz, r_hT[0]], writes=[r_ps[pz]], inc=(kt == KT - 1))
                    op("act", lambda e: e.activation(out=szb, in_=PS[pz], func=AF.Silu), reads=[r_ps[pz]], writes=[r_sz])
                    op("dve", lambda e: e.tensor_tensor(out=obb, in0=t4.rearrange("p h e -> p (h e)"), in1=szb, op=ALU.mult), reads=[r_t4, r_sz], writes=[r_obb])
                    pt = nb()
                    for h in range(4):
                        op("pe", lambda e, h=h: e.transpose(PSB[pt][:, h * P:(h + 1) * P], obb[:, h * P:(h + 1) * P], identb), reads=[r_obb, r_const], writes=[r_ps[pt]], inc=(h == 3))
                    op("act", lambda e: e.activation(out=obT, in_=v4b(pt), func=AF.Copy), reads=[r_ps[pt]], writes=[r_obT])
                    S.dma("sp", dstv[:, :, c * P:(c + 1) * P], obT, reads=[r_obT], writes=[r_obr[1]])
        S.barrier()
        op("dve", lambda e: e.memset(hT[1][:, :, 0:PAD], 0.0), writes=[r_hT[1]])
        op("dve", lambda e: e.memset(hT[1][:, :, PAD + SEQ:HTW], 0.0), writes=[r_hT[1]])
        S.barrier()
        ar.release(m)

    PROF = os.environ.get("KPROF") == "1"

    def scoped(name, fn, *a):
        if PROF:
            with nc.named_scope(name):
                fn(*a)
        else:
            fn(*a)
    for l in range(n_layers):
        scoped(f"mod{l}", phase_mod, l)
        src = x_in if l == 0 else out
        scoped(f"norm{l}", phase_norm_from_dram, src, mod[:, 1, :], mod[:, 0, :], hT[0], r_hT[0])
        if "A" in parts:
            scoped(f"A{l}", phase_A, l)
        else:
            zero_branch(0)
        if "B" in parts:
            scoped(f"B{l}", phase_B, l)
        else:
            zero_branch(1)
        if "C" in parts:
            scoped(f"C{l}", phase_C, l)
        else:
            zero_branch(2)
        scoped(f"merge{l}", phase_merge, l, src)
        scoped(f"ffn{l}", phase_ffn, l, hT[1], r_hT[1], r_out_tiles, True)
    S.barrier()
    S.finish()
    print("nins", S.nins, "nwaits", S.nwaits, "sbuf", ar.off)
    return nc


def _host_inputs(inputs, n_cores=8):
    f = lambda a: np.ascontiguousarray(np.asarray(a))
    x = f(inputs["x"]); c = f(inputs["c"]); pos = f(inputs["positions"]).astype(np.int32)
    w_in = f(inputs["w_in"])
    w_uq = f(inputs["mla_w_uq"])
    Lh = w_uq.shape[0]
    sw = []
    for h in range(4):
        sw.append(w_uq[:, :, h * 192 + 160:h * 192 + 192])
        sw.append(w_uq[:, :, h * 192 + 128:h * 192 + 160])
    w_uqsw = f(np.concatenate(sw, axis=2))
    w_krsw = f(np.concatenate([w_in[:, :, C_KR + 32:C_KR + 64], w_in[:, :, C_KR:C_KR + 32]], axis=2))
    w_spT = f(np.transpose(np.asarray(inputs["a_w_sp"]), (0, 1, 3, 2)))
    colp = np.zeros((Lh, P, NCOL), np.float32)
    colp[:, :, CP_QNW:CP_QNW + 3] = np.asarray(inputs["mla_q_norm_w"]).reshape(Lh, 3, P).transpose(0, 2, 1)
    colp[:, :, CP_KVNW:CP_KVNW + 2] = np.asarray(inputs["mla_kv_norm_w"]).reshape(Lh, 2, P).transpose(0, 2, 1)
    colp[:, :, CP_DNCONV:CP_DNCONV + 60] = np.asarray(inputs["dn_conv_w"]).reshape(Lh, 5, 12, P).transpose(0, 3, 2, 1).reshape(Lh, P, 60)
    colp[:, :, CP_FCW:CP_FCW + 132] = np.asarray(inputs["ffn_conv_w"]).reshape(Lh, 3, 44, P).transpose(0, 3, 2, 1).reshape(Lh, P, 132)
    colp[:, :, CP_FCB:CP_FCB + 44] = np.asarray(inputs["ffn_conv_b"]).reshape(Lh, 44, P).transpose(0, 2, 1)
    colp[:, :, CP_BGATE:CP_BGATE + 24] = np.asarray(inputs["b_gate"]).reshape(Lh, 3, 8, P).transpose(0, 3, 1, 2).reshape(Lh, P, 24)
    colp[:, :, CP_BSP:CP_BSP + 4] = np.asarray(inputs["a_b_sp"]).transpose(0, 2, 1)
    rowp = np.zeros((Lh, 1, NROW), np.float32)
    rowp[:, 0, RP_NW:RP_NW + 4096] = np.asarray(inputs["norm_w"]).reshape(Lh, 4096)
    rowp[:, 0, RP_BADA:RP_BADA + 6144] = np.asarray(inputs["b_ada"])
    rowp[:, 0, RP_LNW:RP_LNW + 512] = np.asarray(inputs["a_ln_w"])
    rowp[:, 0, RP_LNB:RP_LNB + 512] = np.asarray(inputs["a_ln_b"])
    rowp[:, 0, RP_ONW:RP_ONW + 128] = np.asarray(inputs["dn_o_norm_w"])
    rowp[:, 0, RP_ALOG:RP_ALOG + 8] = np.asarray(inputs["dn_a_log"]).reshape(Lh, 8)
    rowp[:, 0, RP_DTB:RP_DTB + 8] = np.asarray(inputs["dn_dt_bias"]).reshape(Lh, 8)
    cst = np.zeros((P, 388), np.float32)
    ii = np.arange(P)[:, None]; jj = np.arange(P)[None, :]
    cst[:, 4:132] = (ii // 32 == jj // 32)
    cst[:, 132:260] = (ii // 64 == jj // 64) & (ii // 32 != jj // 32)
    cst[:, 260:388] = (ii // 64 != jj // 64)
    inv_freq = (10000.0 ** (-np.arange(0, 64, 2, dtype=np.float32) / np.float32(64))).astype(np.float32)
    cst[:, 0] = np.tile(inv_freq, 4)
    cst[:, 1] = np.where((np.arange(P) % 64) < 32, -1.0, 1.0)
    shared = {"w_ada": f(inputs["w_ada"]), "w_in": w_in, "w_uq": w_uq, "w_uqsw": w_uqsw, "w_krsw": w_krsw,
              "w_ukv": f(inputs["mla_w_ukv"]), "w_branch": f(inputs["w_branch"]), "w_o": f(inputs["w_o"]),
              "w_up": f(inputs["ffn_w_up"]), "w_down": f(inputs["ffn_w_down"]), "w_spT": w_spT,
              "colp": colp, "rowp": rowp, "cst": cst}
    maps = []
    for b in range(n_cores):
        mcore = dict(shared)
        mcore["x"] = f(x[b])
        mcore["cT"] = f(c[b].reshape(KT, P).T)
        mcore["pos"] = f(pos[b].reshape(1, SEQ))
        maps.append(mcore)
    return maps


def kernel(**inputs):
    maps = _host_inputs(inputs, 8)
    nc = build()
    res = run_bass_kernel_spmd(nc, maps, core_ids=list(range(8)))
    return np.stack([r["out"] for r in res.results], axis=0).astype(np.float32)
```
